# Optimizing a Trainium2 kernel written in Bass

```python
import jax, jax.numpy as jnp
from jax import lax
import numpy as np

D_MODEL = 1024
BATCH = 4
SEQ = 4096
DEPTH = 1

D_MIX = 2 * D_MODEL
D_POOL = D_MIX // 2
D_MLSTM = D_MIX - D_POOL
POOL_WINDOWS = (2, 4, 8, 16)
N_POOL_GROUPS = len(POOL_WINDOWS)
POOL_GROUP_DIM = D_POOL // N_POOL_GROUPS
N_HEADS = 4
HEAD_DIM = D_MLSTM // N_HEADS
QKV_BLOCK = 4
N_QKV_BLOCKS = D_MLSTM // QKV_BLOCK
CONV_WIDTH = 5
CHUNK = 128
N_DIRS = 2
EPS = 1e-6
NEG_INF = -1e30

kernel_name = "hymba_pool_mlstm_bidir_block"


def rms_norm(x, g):
    xf = x.astype(jnp.float32)
    y = xf * lax.rsqrt(jnp.mean(xf * xf, axis=-1, keepdims=True) + EPS)
    return (y * g.astype(jnp.float32)).astype(x.dtype)


def window_mean_minus_self(u, w):
    S = u.shape[1]
    left = (w - 1) // 2
    right = w - 1 - left
    uf = u.astype(jnp.float32)
    cs = jnp.concatenate([jnp.zeros_like(uf[:, :1]), jnp.cumsum(uf, axis=1)], axis=1)
    t = np.arange(S)
    lo = np.maximum(t - left, 0)
    hi = np.minimum(t + right, S - 1)
    total = cs[:, hi + 1] - cs[:, lo]
    count = (hi - lo + 1).astype(np.float32)[None, :, None]
    return (total / count - uf).astype(u.dtype)


def pool_mixer(u, pool_w, pool_scale):
    B, S, _ = u.shape
    ug = u.reshape(B, S, N_POOL_GROUPS, POOL_GROUP_DIM)
    pooled = jnp.stack([window_mean_minus_self(ug[:, :, g], w) for g, w in enumerate(POOL_WINDOWS)], axis=2)
    mixed = jnp.einsum('bsgc,gcd->bsgd', pooled, pool_w).reshape(B, S, D_POOL)
    return mixed * pool_scale


def centred_depthwise_conv(u, w, b):
    K, C = w.shape
    pad = K // 2
    y = lax.conv_general_dilated(u, w[:, None, :].astype(u.dtype), window_strides=(1,),
                                 padding=[(pad, pad)], dimension_numbers=('NWC', 'WIO', 'NWC'),
                                 feature_group_count=C)
    return y + b


def headwise(u, w):
    B, S, _ = u.shape
    return jnp.einsum('bsni,nio->bsno', u.reshape(B, S, w.shape[0], w.shape[1]), w).reshape(B, S, -1)


def mlstm_chunkwise(q, k, v, log_i, log_f):
    B, H, S, d = q.shape
    nc = S // CHUNK

    def to_chunks(a):
        return jnp.moveaxis(a.reshape(B, H, nc, CHUNK, *a.shape[3:]), 2, 0)

    qc, kc, vc, ic, fc = map(to_chunks, (q, k, v, log_i, log_f))
    lower_tri = jnp.tril(jnp.ones((CHUNK, CHUNK), dtype=bool))

    def step(carry, xs):
        C, n, m = carry
        qj, kj, vj, ij, fj = xs
        b = jnp.cumsum(fj, axis=-1)
        log_inter = b + m[..., None]
        log_D = b[..., :, None] - b[..., None, :] + ij[..., None, :]
        log_D = jnp.where(lower_tri, log_D, NEG_INF)
        m_t = jnp.maximum(log_inter, jnp.max(log_D, axis=-1))
        inter_w = jnp.exp(log_inter - m_t)
        s = jnp.einsum('bhtd,bhsd->bhts', qj, kj) * jnp.exp(log_D - m_t[..., None])
        num = jnp.einsum('bhts,bhse->bhte', s, vj) + inter_w[..., None] * jnp.einsum('bhtd,bhde->bhte', qj, C)
        den = jnp.sum(s, axis=-1) + inter_w * jnp.einsum('bhtd,bhd->bht', qj, n)
        h = num / jnp.maximum(jnp.abs(den), jnp.exp(-m_t))[..., None]
        bL = b[..., -1]
        log_s = bL[..., None] - b + ij
        m_new = jnp.maximum(bL + m, jnp.max(log_s, axis=-1))
        decay = jnp.exp(bL + m - m_new)
        ws = jnp.exp(log_s - m_new[..., None])
        C_new = decay[..., None, None] * C + jnp.einsum('bhs,bhsd,bhse->bhde', ws, kj, vj)
        n_new = decay[..., None] * n + jnp.einsum('bhs,bhsd->bhd', ws, kj)
        return (C_new, n_new, m_new), h

    init = (jnp.zeros((B, H, d, d), jnp.float32), jnp.zeros((B, H, d), jnp.float32),
            jnp.zeros((B, H), jnp.float32))
    _, hs = lax.scan(step, init, (qc, kc, vc, ic, fc))
    return jnp.moveaxis(hs, 0, 2).reshape(B, H, S, d)


def mlstm_branch(u, conv_w, conv_b, w_q, w_k, w_v, w_gates, b_gates, mh_norm_w, skip_w):
    B, S, _ = u.shape
    c = jax.nn.silu(centred_depthwise_conv(u, conv_w, conv_b))
    q = headwise(c, w_q)
    k = headwise(c, w_k)
    v = headwise(u, w_v)
    qkv = jnp.concatenate([q, k, v], axis=-1).astype(jnp.float32)
    gates = jnp.einsum('bsc,ncg->nbsg', qkv, w_gates.astype(jnp.float32)) \
        + b_gates.astype(jnp.float32)[:, None, None, :]
    gates = jnp.transpose(gates, (0, 1, 3, 2))
    log_i = gates[:, :, :N_HEADS]
    log_f = jax.nn.log_sigmoid(gates[:, :, N_HEADS:])

    def heads(a):
        return a.reshape(B, S, N_HEADS, HEAD_DIM).transpose(0, 2, 1, 3).astype(jnp.float32)

    qh, kh, vh = heads(q), heads(k) * (HEAD_DIM ** -0.5), heads(v)
    h_fwd = mlstm_chunkwise(qh, kh, vh, log_i[0], log_f[0])
    h_bwd = jnp.flip(mlstm_chunkwise(jnp.flip(qh, 2), jnp.flip(kh, 2), jnp.flip(vh, 2),
                                     jnp.flip(log_i[1], -1), jnp.flip(log_f[1], -1)), 2)
    h = h_fwd + h_bwd
    mu = jnp.mean(h, axis=-1, keepdims=True)
    var = jnp.mean(jnp.square(h - mu), axis=-1, keepdims=True)
    h = (h - mu) * lax.rsqrt(var + EPS)
    h = h.transpose(0, 2, 1, 3).reshape(B, S, D_MLSTM)
    out = h * mh_norm_w.astype(jnp.float32) + skip_w.astype(jnp.float32) * c.astype(jnp.float32)
    return out.astype(u.dtype)


def setup_inputs(seed: int = 0) -> dict:
    key = jax.random.key(seed)
    ks = jax.random.split(key, 20)
    nrm = jax.random.normal
    f32 = jnp.float32
    x = nrm(ks[0], (BATCH, SEQ, D_MODEL), f32)
    norm_in_g = 1.0 + 0.1 * nrm(ks[1], (DEPTH, D_MODEL), f32)
    w_in = nrm(ks[2], (DEPTH, D_MODEL, 2 * D_MIX), f32) * D_MODEL ** -0.5
    pool_w = nrm(ks[3], (DEPTH, N_POOL_GROUPS, POOL_GROUP_DIM, POOL_GROUP_DIM), f32) * POOL_GROUP_DIM ** -0.5
    pool_scale = 1.0 + 0.1 * nrm(ks[4], (DEPTH, D_POOL), f32)
    conv_w = nrm(ks[5], (DEPTH, CONV_WIDTH, D_MLSTM), f32) * CONV_WIDTH ** -0.5
    conv_b = 0.01 * nrm(ks[6], (DEPTH, D_MLSTM), f32)
    w_q = nrm(ks[7], (DEPTH, N_QKV_BLOCKS, QKV_BLOCK, QKV_BLOCK), f32) * QKV_BLOCK ** -0.5
    w_k = nrm(ks[8], (DEPTH, N_QKV_BLOCKS, QKV_BLOCK, QKV_BLOCK), f32) * QKV_BLOCK ** -0.5
    w_v = nrm(ks[9], (DEPTH, N_QKV_BLOCKS, QKV_BLOCK, QKV_BLOCK), f32) * QKV_BLOCK ** -0.5
    w_gates = nrm(ks[10], (DEPTH, N_DIRS, 3 * D_MLSTM, 2 * N_HEADS), f32) * (3 * D_MLSTM) ** -0.5
    b_i = 0.1 * nrm(ks[11], (DEPTH, N_DIRS, N_HEADS), f32)
    b_f = jnp.linspace(3.0, 6.0, N_HEADS, dtype=f32) + 0.1 * nrm(ks[12], (DEPTH, N_DIRS, N_HEADS), f32)
    b_gates = jnp.concatenate([b_i, b_f], axis=-1)
    mh_norm_w = 1.0 + 0.1 * nrm(ks[13], (DEPTH, D_MLSTM), f32)
    skip_w = 1.0 + 0.1 * nrm(ks[14], (DEPTH, D_MLSTM), f32)
    w_out = nrm(ks[15], (DEPTH, D_MIX, D_MODEL), f32) * D_MIX ** -0.5
    norm_out_g = 1.0 + 0.1 * nrm(ks[16], (D_MODEL,), f32)
    return {"x": x, "norm_in_g": norm_in_g, "w_in": w_in, "pool_w": pool_w, "pool_scale": pool_scale,
            "conv_w": conv_w, "conv_b": conv_b, "w_q": w_q, "w_k": w_k, "w_v": w_v,
            "w_gates": w_gates, "b_gates": b_gates, "mh_norm_w": mh_norm_w, "skip_w": skip_w,
            "w_out": w_out, "norm_out_g": norm_out_g}


def reference(x, norm_in_g, w_in, pool_w, pool_scale, conv_w, conv_b, w_q, w_k, w_v,
              w_gates, b_gates, mh_norm_w, skip_w, w_out, norm_out_g):
    h = x
    for l in range(DEPTH):
        u = rms_norm(h, norm_in_g[l])
        proj = jnp.einsum('bsd,de->bse', u, w_in[l])
        pool_x, pool_z, m_x, m_z = jnp.split(proj, [D_POOL, 2 * D_POOL, 2 * D_POOL + D_MLSTM], axis=-1)
        y_pool = pool_mixer(pool_x, pool_w[l], pool_scale[l]) * jax.nn.silu(pool_z)
        y_m = mlstm_branch(m_x, conv_w[l], conv_b[l], w_q[l], w_k[l], w_v[l], w_gates[l],
                           b_gates[l], mh_norm_w[l], skip_w[l]) * jax.nn.silu(m_z)
        y = jnp.concatenate([y_pool.astype(h.dtype), y_m.astype(h.dtype)], axis=-1)
        h = h + jnp.einsum('bse,ed->bsd', y, w_out[l])
    return rms_norm(h, norm_out_g)
```

```python
import numpy as np
from contextlib import ExitStack
import concourse.bass as bass
import concourse.mybir as mybir
from concourse.bass_utils import run_bass_kernel_spmd

F32 = mybir.dt.float32
BF16 = mybir.dt.bfloat16
AF = mybir.ActivationFunctionType
ALU = mybir.AluOpType

D = 1024
SEQ = 4096
NOWN = 2048
EPS = 1e-6
POOL_WINDOWS = (2, 4, 8, 16)
DEBUG = False
OP_LIMIT = None
FILL_EVERY = 0


class Res:
    __slots__ = ("last_w", "readers", "name")

    def __init__(self, name=""):
        self.last_w = None
        self.readers = {}
        self.name = name


class Sched:
    SAME_ENGINE_SYNC = True

    def __init__(self, nc, ndma=24):
        self.nc = nc
        self.engs = {"pe": nc.tensor, "act": nc.scalar, "dve": nc.vector, "pool": nc.gpsimd, "sp": nc.sync}
        self.sem = {k: nc.alloc_semaphore(name=f"s_{k}") for k in self.engs}
        self.count = {k: 0 for k in self.engs}
        self.waited = {k: {} for k in self.engs}
        self.snaps = {k: [] for k in self.engs}
        self.dma_snap = {}
        self.TRANSITIVE = True
        self.dma_sems = [nc.alloc_semaphore(name=f"s_dma{i}") for i in range(ndma)]
        self.dma_cnt = [0] * ndma
        self.n_sw = 6
        self.dma_next = {"sw": 0, "hw": 0}
        self.nwaits = 0
        self.nops = 0
        self.limit = None
        self.log = []

    def _wait(self, e, tok, raw=False):
        kind, x, c = tok
        key = (kind, x)
        if kind == "eng" and x == e and (e == "pe" or not (self.SAME_ENGINE_SYNC or raw)):
            return
        if self.waited[e].get(key, 0) >= c:
            return
        sem = self.sem[x] if kind == "eng" else self.dma_sems[x]
        self.engs[e].wait_ge(sem, c)
        self.nwaits += 1
        self.waited[e][key] = c
        if self.TRANSITIVE:
            if kind == "eng":
                snap = None
                for cnt, kn in reversed(self.snaps[x]):
                    if cnt <= c:
                        snap = kn
                        break
            else:
                snap = self.dma_snap.get((x, c))
            if snap:
                w = self.waited[e]
                for k2, v2 in snap.items():
                    if w.get(k2, 0) < v2:
                        w[k2] = v2

    def _deps(self, e, reads, writes):
        for r in reads:
            if r.last_w is not None:
                self._wait(e, r.last_w, raw=True)
        for w in writes:
            if w.last_w is not None:
                self._wait(e, w.last_w)
            for d in w.readers.values():
                self._wait(e, d)

    def _commit(self, tok, reads, writes):
        for r in reads:
            r.readers[(tok[0], tok[1])] = tok
        for w in writes:
            w.last_w = tok
            w.readers = {}

    def op(self, e, emit, reads=(), writes=()):
        if self.limit is not None and self.nops >= self.limit:
            return None
        self._deps(e, reads, writes)
        inst = emit(self.engs[e])
        self.count[e] += 1
        self.nops += 1
        if DEBUG:
            import sys
            self.log.append((self.nops, e, sys._getframe(1).f_lineno))
        inst.then_inc(self.sem[e], 1)
        if self.TRANSITIVE and (not self.snaps[e] or self.snaps[e][-1][1] != self.waited[e]):
            self.snaps[e].append((self.count[e], dict(self.waited[e])))
        tok = ("eng", e, self.count[e])
        self._commit(tok, reads, writes)
        return tok

    def dma(self, e, out, in_, reads=(), writes=(), **kw):
        if self.limit is not None and self.nops >= self.limit:
            return None
        self.nops += 1
        if DEBUG:
            import sys
            self.log.append((self.nops, "dma-" + e, sys._getframe(1).f_lineno))
        if e == "pool":
            kw.setdefault("max_dma_last_dim", 4096)
        self._deps(e, reads, writes)
        if e == "pool":
            k = self.dma_next["sw"]
            self.dma_next["sw"] = (k + 1) % self.n_sw
        else:
            k = self.n_sw + self.dma_next["hw"]
            self.dma_next["hw"] = (self.dma_next["hw"] + 1) % (len(self.dma_sems) - self.n_sw)
        if self.dma_cnt[k] > 0:
            self._wait(e, ("dma", k, 16 * self.dma_cnt[k]))
        inst = self.engs[e].dma_start(out=out, in_=in_, **kw)
        self.dma_cnt[k] += 1
        inst.then_inc(self.dma_sems[k], 16)
        if self.TRANSITIVE:
            self.dma_snap[(k, 16 * self.dma_cnt[k])] = dict(self.waited[e])
        tok = ("dma", k, 16 * self.dma_cnt[k])
        self._commit(tok, reads, writes)
        return tok

    def barrier(self):
        for e in self.engs:
            for x in self.engs:
                if x != e and self.count[x] > 0:
                    self._wait(e, ("eng", x, self.count[x]))
            for k in range(len(self.dma_sems)):
                if self.dma_cnt[k] > 0:
                    self._wait(e, ("dma", k, 16 * self.dma_cnt[k]))

    def finish(self, e, resources):
        for r in resources:
            if r.last_w is not None:
                self._wait(e, r.last_w)


class Tl:
    def __init__(self, t, name=""):
        self.t = t
        self.r = Res(name)
        self.subs = {}

    def sub(self, k):
        if k not in self.subs:
            r = Res(f"{self.r.name}.{k}")
            r.last_w = self.r.last_w
            r.readers = dict(self.r.readers)
            self.subs[k] = r
        return self.subs[k]

    def all(self):
        return [self.r] + list(self.subs.values())


def build_program(stop_after=None):
    nc = bass.Bass("TRN2", target_bir_lowering=False)

    def din(name, shape, dt=F32):
        return nc.dram_tensor(name, shape, dt, kind="ExternalInput").ap()

    xl = din("xl", [SEQ, D])
    w_in = din("w_in", [D, 4 * D])
    w_out = din("w_out", [2 * D, D])
    pool_w = din("pool_w", [1024, 256])
    cdiag = din("cdiag", [128, 40 * 128])
    bd = din("bd", [128, 3 * 8 * 128])
    bdT = din("bdT", [128, 3 * 8 * 128])
    wg = din("wg", [128, 24 * 16])
    vecs = din("vecs", [128, 4 * 8])
    bcast = din("bcast", [128, 2 * D + 16])
    cst = din("cst", [128, 3 * 128])
    ident = din("ident", [128, 128])
    bands = din("bands", [128, 16 * 128])
    out = nc.dram_tensor("out", [NOWN, D], F32, kind="ExternalOutput").ap()

    skind = "ExternalOutput" if DEBUG else "Internal"

    def dscr(name, shape, dt):
        return nc.dram_tensor(name, shape, dt, kind=skind).ap()

    QT_s = dscr("QT_s", [128, 8, NOWN], BF16)
    KT_s = dscr("KT_s", [128, 8, NOWN], BF16)
    SC_s = dscr("SC_s", [128, 8, NOWN], BF16)
    SMZ_s = dscr("SMZ_s", [128, 8, NOWN], BF16)
    YP_s = dscr("YP_s", [128, 8, NOWN], BF16)
    UT_s = dscr("UT_s", [128, 8, SEQ], BF16)
    ut_res = [Res(f"ut_s{b}") for b in range(8)]
    K_s = dscr("K_s", [16, 128, D], BF16)
    V_s = dscr("V_s", [16, 128, D], BF16)
    HF_s = dscr("HF_s", [16, 128, D], F32)
    GS_s = dscr("GS_s", [128, 32 * 32], F32) if DEBUG else None
    r_scr = {n: Res(n) for n in ["QT", "KT", "SC", "SMZ", "YP", "K", "V", "HF", "out", "GS", "UT"]}

    S = Sched(nc)
    S.limit = OP_LIMIT
    glob = ExitStack()

    uniq = [0]

    def sb(stack, name, shape, dt):
        uniq[0] += 1
        return Tl(stack.enter_context(nc.sbuf_tensor(f"sb{uniq[0]}_{name}", shape, dt)), name)

    pstate = {"stack": None, "n": 0}
    TP, MM, SA, SB, PD = [], [], [], [], []

    def set_psum(tp, mm, scan, sa=0):
        if pstate["stack"] is not None:
            pstate["stack"].close()
        st = pstate["stack"] = ExitStack()

        def psum(name, shape, dt):
            pstate["n"] += 1
            return Tl(st.enter_context(nc.psum_tensor(f"ps{pstate['n']}_{name}", shape, dt)), name)

        TP[:] = [psum(f"tp{i}", [128, 8, 128], BF16) for i in range(tp)]
        MM[:] = [psum(f"mm{i}", [128, 512], F32) for i in range(mm)]
        SA[:] = [psum(f"sa{i}", [128, 512], F32) for i in range(sa)] if scan else []
        SB[:] = [psum(f"sb{i}", [128, 512], F32) for i in range(2)] if scan else []
        PD[:] = [psum(f"pd{i}", [128, 512], F32) for i in range(2)] if scan else []

    set_psum(tp=1, mm=7, scan=False)
    cnt = {"tp": 0, "mm": 0}

    def next_tp():
        cnt["tp"] += 1
        return TP[cnt["tp"] % len(TP)]

    def next_mm():
        cnt["mm"] += 1
        return MM[cnt["mm"] % len(MM)]

    fill = {"n": 0, "every": FILL_EVERY, "cnt": 0}

    def keep_warm():
        if fill["every"] <= 0 or (S.limit is not None and S.nops >= S.limit):
            return
        fill["cnt"] += 1
        if fill["cnt"] % fill["every"] == 0:
            pass
            fill["n"] += 1

    def ilv(*gens):
        active = [g for g in gens if g is not None]
        while active:
            for g in list(active):
                try:
                    next(g)
                    keep_warm()
                    yield
                except StopIteration:
                    active.remove(g)

    def ilvw(*pairs):
        active = [[g, w] for g, w in pairs if g is not None]
        while active:
            for item in list(active):
                g, w = item
                for _ in range(w):
                    try:
                        next(g)
                        yield
                    except StopIteration:
                        active.remove(item)
                        break

    def seq(*gens):
        for g in gens:
            if g is not None:
                yield from g

    def run(g):
        for _ in g:
            pass

    CST = sb(glob, "cst", [128, 3 * 128], F32)
    IDB = sb(glob, "idb", [128, 128], BF16)
    BC = sb(glob, "bc", [128, 2 * D + 16], F32)
    VEC = sb(glob, "vec", [128, 32], F32)
    GS = sb(glob, "gs", [128, 32 * 32], F32)
    MH = sb(glob, "mh", [128, 4], F32)
    SS = sb(glob, "ss", [128, 8], F32)
    JUNK = sb(glob, "junk", [128, D], BF16)
    gs_res = [{k: Res(f"gs{t}{k}") for k in ("as", "ae", "fl", "dec")} for t in range(32)]

    S.dma("sp", CST.t[:], cst[:, :], writes=[CST.r])
    S.dma("pool", IDB.t[:], ident[:, :], writes=[IDB.r])
    S.dma("sp", BC.t[:], bcast[:, :], writes=[BC.r])
    S.dma("sp", VEC.t[:], vecs[:, :], writes=[VEC.r])
    S.op("pool", lambda e: e.memset(MH.t[:], -0.5), writes=[MH.r])
    LF = CST.t[:, 0:128]
    LR = CST.t[:, 128:256]
    ONES = CST.t[:, 256:384]
    G_IN = BC.t[:, 0:D]
    G_OUT = BC.t[:, D:2 * D]
    GBIAS = BC.t[:, 2 * D:2 * D + 16]
    PSCALE = lambda cc: VEC.t[:, cc:cc + 1]
    CONVB = lambda cc: VEC.t[:, 8 + cc:9 + cc]
    WN = lambda cc: VEC.t[:, 16 + cc:17 + cc]
    SKIP = lambda cc: VEC.t[:, 24 + cc:25 + cc]

    w_in_v = w_in.rearrange("(kc p) c -> p kc c", p=128)

    def rms_scale(src, col):
        S.op("act", lambda e: e.activation(out=JUNK.t[:], in_=src.t[:], func=AF.Square, accum_out=SS.t[:, col:col + 1]),
             reads=src.all() + [SS.r], writes=[JUNK.r, SS.r])
        S.op("pool", lambda e: e.tensor_scalar(out=SS.t[:, col + 1:col + 2], in0=SS.t[:, col:col + 1], scalar1=1.0 / D,
                                                scalar2=EPS, op0=ALU.mult, op1=ALU.add), reads=[SS.r], writes=[SS.r])
        S.op("pool", lambda e: e.tensor_tensor(out=SS.t[:, col + 2:col + 3], in0=SS.t[:, col + 1:col + 2],
                                                in1=MH.t[:, 0:1], op=ALU.pow), reads=[SS.r, MH.r], writes=[SS.r])

    SSR = sb(glob, "ssr", [128, 16], F32)
    ssr_res = [Res(f"ssr{i}") for i in range(4)]

    class XPipe:
        def __init__(self, XT, XN, tiles, LA=2, LD=3):
            self.XT, self.XN, self.tiles, self.LA, self.LD = XT, XN, tiles, LA, LD
            self.i_a = self.i_d = 0
            assert len(XT) >= LD + 2

        def ahead_dma(self, upto):
            while self.i_d <= min(upto, len(self.tiles) - 1):
                idx = self.i_d
                xt = self.XT[idx % len(self.XT)]
                T = self.tiles[idx]
                S.dma("act", xt.t[:], xl[T * 128:(T + 1) * 128, :], writes=[xt.r])
                self.i_d += 1

        def ahead_stats(self, upto):
            while self.i_a <= min(upto, len(self.tiles) - 1):
                idx = self.i_a
                self.ahead_dma(idx)
                xt = self.XT[idx % len(self.XT)]
                c = (idx % 4) * 4
                r = ssr_res[idx % 4]
                S.op("act", lambda e: e.activation(out=JUNK.t[:], in_=xt.t[:], func=AF.Square, accum_out=SSR.t[:, c:c + 1]),
                     reads=[xt.r, r], writes=[JUNK.r, r])
                S.op("pool", lambda e: e.tensor_scalar(out=SSR.t[:, c + 1:c + 2], in0=SSR.t[:, c:c + 1], scalar1=1.0 / D,
                                                        scalar2=EPS, op0=ALU.mult, op1=ALU.add), reads=[r], writes=[r])
                S.op("pool", lambda e: e.tensor_tensor(out=SSR.t[:, c + 2:c + 3], in0=SSR.t[:, c + 1:c + 2],
                                                        in1=MH.t[:, 0:1], op=ALU.pow), reads=[r, MH.r], writes=[r])
                self.i_a += 1

        def prime(self):
            self.ahead_dma(self.LD)
            self.ahead_stats(self.LA)

        def get_xn(self, idx):
            self.ahead_stats(idx)
            xt = self.XT[idx % len(self.XT)]
            xn = self.XN[idx % len(self.XN)]
            c = (idx % 4) * 4
            S.op("dve", lambda e: e.scalar_tensor_tensor(out=xn.t[:], in0=xt.t[:], scalar=SSR.t[:, c + 2:c + 3], in1=G_IN,
                                                          op0=ALU.mult, op1=ALU.mult),
                 reads=[xt.r, ssr_res[idx % 4], BC.r], writes=[xn.r])
            return xn

        def after(self, idx):
            self.ahead_dma(idx + self.LD)
            self.ahead_stats(idx + self.LA)

    def make_uT(pipe, idx0, ntiles, UT):
        for i in range(ntiles):
            xn = pipe.get_xn(idx0 + i)
            yield
            tp = next_tp()
            for kc in range(8):
                S.op("pe", lambda e: e.transpose(tp.t[:, kc, :], xn.t[:, kc * 128:(kc + 1) * 128], IDB.t[:]),
                     reads=[xn.r, IDB.r], writes=[tp.r])
            if (idx0 + i) % 2 == 0:
                S.op("act", lambda e: e.activation(out=UT.t[:, :, i * 128:(i + 1) * 128], in_=tp.t[:, :, :], func=AF.Copy),
                     reads=[tp.r], writes=[UT.r])
            else:
                S.op("dve", lambda e: e.tensor_copy(out=UT.t[:, :, i * 128:(i + 1) * 128], in_=tp.t[:, :, :]),
                     reads=[tp.r], writes=[UT.r])
            pipe.after(idx0 + i)
            yield

    def proj_fm(W, wcol0, ncc, UT, ntok, evac, fine=False, w_r=None):
        for cc in range(ncc):
            ps = next_mm()
            for kc in range(8):
                S.op("pe", lambda e: e.matmul(ps.t[:, 0:ntok], lhsT=W.t[:, kc, wcol0 + cc * 128: wcol0 + (cc + 1) * 128],
                                              rhs=UT.t[:, kc, 0:ntok], start=(kc == 0), stop=(kc == 7)),
                     reads=[w_r or W.r, UT.r], writes=[ps.r])
                if fine and kc % 2 == 1 and kc < 7:
                    yield
            evac(cc, ps)
            yield

    abw = ExitStack()
    WMX = sb(abw, "w_mx", [128, 8, D], BF16)
    CD = sb(abw, "cd", [128, 40, 128], BF16)
    BD = sb(abw, "bd", [128, 24, 128], BF16)
    WF = sb(abw, "wf", [128, 16, 16], BF16)
    ph = ExitStack()
    WR = sb(ph, "w_rest", [128, 8, 3 * D], BF16)
    PW = sb(ph, "pool_w", [128, 8, 256], BF16)
    BND = sb(ph, "bands", [128, 16, 128], BF16)
    XT = [sb(ph, f"xt{i}", [128, D], F32) for i in range(5)]
    XN = [sb(ph, f"xn{i}", [128, D], BF16) for i in range(2)]
    UTD = [sb(ph, f"ut{i}", [128, 8, 512], BF16) for i in range(2)]
    p0_blocks = [3, 4, 5, 6, 7, 0, 1, 2]
    xpipe = XPipe(XT, XN, [b * 4 + i for b in p0_blocks for i in range(4)])
    xpipe.prime()
    NPX = 8
    PX = [sb(ph, f"px{i}", [128, D], BF16) for i in range(NPX)]
    SPZ = [sb(ph, f"spz{i}", [128, 8, 512], BF16) for i in range(2)]
    SMZ = [sb(ph, f"smz{i}", [128, 8, 512], BF16) for i in range(1)]
    PL = [sb(ph, f"pl{i}", [128, 8, 128], BF16) for i in range(2)]
    YPT = [sb(ph, f"ypt{i}", [128, 8, 128], BF16) for i in range(2)]


    def px_tile(UTb, i, T):
        pxt = PX[T % NPX]
        for cg in range(2):
            ps = next_mm()
            for kc in range(8):
                S.op("pe", lambda e: e.matmul(ps.t[:, 0:512], lhsT=UTb.t[:, kc, i * 128:(i + 1) * 128],
                                              rhs=WR.t[:, kc, cg * 512:(cg + 1) * 512], start=(kc == 0), stop=(kc == 7)),
                     reads=[UTb.r, WR.sub(0)], writes=[ps.r])
            S.op("dve", lambda e: e.tensor_copy(out=pxt.t[:, cg * 512:(cg + 1) * 512], in_=ps.t[:, 0:512]),
                 reads=[ps.r], writes=[pxt.r])
            yield

    def pool_tile(T, last):
        pl = PL[T % 2]
        ypt = YPT[T % 2]
        spz = SPZ[((T - 16) // 4) % 2]
        tcol = ((T - 16) % 4) * 128
        for half in range(2):
            ps = next_mm()
            for c4 in range(4):
                cc = half * 4 + c4
                g = cc // 2
                srcs = [(PX[(T - 1) % NPX], 0), (PX[T % NPX], 3 if last else 1)]
                if not last:
                    srcs.append((PX[(T + 1) % NPX], 2))
                for n, (pxs, kind) in enumerate(srcs):
                    S.op("pe", lambda e: e.matmul(ps.t[:, c4 * 128:(c4 + 1) * 128], lhsT=pxs.t[:, cc * 128:(cc + 1) * 128],
                                                  rhs=BND.t[:, g * 4 + kind, :], start=(n == 0), stop=(n == len(srcs) - 1)),
                         reads=[pxs.r, BND.r], writes=[ps.r])
            S.op("act", lambda e: e.activation(out=pl.t[:, half * 4:(half + 1) * 4, :],
                                               in_=ps.t[:, 0:512].rearrange("p (c t) -> p c t", t=128), func=AF.Copy),
                 reads=[ps.r], writes=[pl.r])
            yield
        for half in range(2):
            ps = next_mm()
            for c4 in range(4):
                oc = half * 4 + c4
                g = oc // 2
                for k2 in range(2):
                    S.op("pe", lambda e: e.matmul(ps.t[:, c4 * 128:(c4 + 1) * 128],
                                                  lhsT=PW.t[:, g * 2 + k2, (oc % 2) * 128:(oc % 2 + 1) * 128],
                                                  rhs=pl.t[:, g * 2 + k2, :], start=(k2 == 0), stop=(k2 == 1)),
                         reads=[PW.r, pl.r], writes=[ps.r])
            for c4 in range(4):
                oc = half * 4 + c4
                S.op("dve", lambda e: e.scalar_tensor_tensor(out=ypt.t[:, oc, :], in0=ps.t[:, c4 * 128:(c4 + 1) * 128],
                                                              scalar=PSCALE(oc), in1=spz.t[:, oc, tcol:tcol + 128],
                                                              op0=ALU.mult, op1=ALU.mult),
                     reads=[ps.r, VEC.r, spz.r], writes=[ypt.r])
            yield
        o0 = (T - 16) * 128
        S.dma("sp", YP_s[:, :, o0:o0 + 128], ypt.t[:, :, :], reads=[ypt.r], writes=[r_scr["YP"]])

    def load_ut(UTbuf, blk, queue="act"):
        S.dma(queue, UTbuf.t[:, :, :], UT_s[:, :, blk * 512:(blk + 1) * 512], reads=[ut_res[blk]], writes=[UTbuf.r])

    def c1_front(ob):
        UT = UTD[ob % 2]
        if ob < 3:
            load_ut(UTD[(ob + 1) % 2], 4 + ob + 1)
        yield
        spz, smz = SPZ[ob % 2], SMZ[0]
        for i in range(4):
            yield from px_tile(UT, i, 16 + ob * 4 + i)
        yield from proj_fm(WR, D, 8, UT, 512, lambda cc, ps: S.op(
            "act", lambda e: e.activation(out=spz.t[:, cc, :], in_=ps.t[:, 0:512], func=AF.Silu), reads=[ps.r], writes=[spz.r]),
            w_r=WR.sub(1))
        yield from proj_fm(WR, 2 * D, 8, UT, 512, lambda cc, ps: S.op(
            "act", lambda e: e.activation(out=smz.t[:, cc, :], in_=ps.t[:, 0:512], func=AF.Silu), reads=[ps.r], writes=[smz.r]),
            w_r=WR.sub(2))
        S.dma("sp", SMZ_s[:, :, ob * 512:(ob + 1) * 512], smz.t[:, :, :], reads=[smz.r], writes=[r_scr["SMZ"]])

    def c1_pool(ob):
        T0 = 16 + ob * 4
        for T in range(T0 - 1, T0 + 3):
            if T >= 16:
                yield from pool_tile(T, False)

    def p0_block(k, buf):
        blk = p0_blocks[k]
        yield from make_uT(xpipe, k * 4, 4, buf)
        S.dma("sp", UT_s[:, :, blk * 512:(blk + 1) * 512], buf.t[:, :, :], reads=[buf.r], writes=[ut_res[blk]])
        yield

    fold_tmp = ExitStack()
    BDT = sb(fold_tmp, "bdT", [128, 24, 128], F32)
    WG = sb(fold_tmp, "wg", [128, 24, 16], F32)
    S.dma("sp", BDT.t[:, :, :], bdT.rearrange("p (k q) -> p k q", q=128), writes=[BDT.r])
    S.dma("sp", WG.t[:, :, :], wg.rearrange("p (k q) -> p k q", q=16), writes=[WG.r])

    def fold_gen():
        GP = next_mm()
        for cc in range(8):
            S.op("pe", lambda e: e.matmul(GP.t[:, cc * 16:(cc + 1) * 16], lhsT=BDT.t[:, cc, :], rhs=WG.t[:, cc, :],
                                          start=True, stop=False), reads=[BDT.r, WG.r], writes=[GP.r])
            S.op("pe", lambda e: e.matmul(GP.t[:, cc * 16:(cc + 1) * 16], lhsT=BDT.t[:, 8 + cc, :], rhs=WG.t[:, 8 + cc, :],
                                          start=False, stop=True), reads=[BDT.r, WG.r], writes=[GP.r])
            S.op("pe", lambda e: e.matmul(GP.t[:, 128 + cc * 16:128 + (cc + 1) * 16], lhsT=BDT.t[:, 16 + cc, :],
                                          rhs=WG.t[:, 16 + cc, :], start=True, stop=True), reads=[BDT.r, WG.r], writes=[GP.r])
            yield
        S.op("dve", lambda e: e.tensor_copy(out=WF.t[:, :, :], in_=GP.t[:, 0:256].rearrange("p (k q) -> p k q", q=16)),
             reads=[GP.r], writes=[WF.r])
        yield

    def wdma(j):
        if j < 3:
            c0 = (0, D, 3 * D)[j]
            S.dma("pool", WR.t[:, :, j * D:(j + 1) * D], w_in_v[:, :, c0:c0 + D], writes=[WR.sub(j)])
        elif j == 3:
            S.dma("pool", PW.t[:, :, :], pool_w.rearrange("(k p) d -> p k d", p=128), writes=[PW.r])
            S.dma("pool", BND.t[:, :, :], bands.rearrange("p (k t) -> p k t", t=128), writes=[BND.r])
        else:
            S.dma("pool", WMX.t[:, :, :], w_in_v[:, :, 2 * D:3 * D], writes=[WMX.r])
            S.dma("pool", CD.t[:, :, :], cdiag.rearrange("p (k q) -> p k q", q=128), writes=[CD.r])
            S.dma("pool", BD.t[:, :, :], bd.rearrange("p (k q) -> p k q", q=128), writes=[BD.r])
        yield

    wdma_seq = [wdma(0), wdma(1), wdma(2), wdma(3), wdma(4)]
    run(wdma_seq[0])
    run(ilv(seq(*[g for k in range(5) for g in (p0_block(k, UTD[k % 2]), wdma_seq[k] if k > 0 else None)]), fold_gen()))
    fold_tmp_res = [BDT.r, WG.r]
    fold_tmp.close()
    UTP = sb(ph, "utp", [128, 8, 512], BF16)
    for r in fold_tmp_res:
        UTP.r.readers.update(r.readers)
        if r.last_w is not None:
            UTP.r.readers[("w", r.name)] = r.last_w
    S.dma("act", UTD[1].t[:, :, 0:128], UT_s[:, :, 15 * 128:16 * 128], reads=[ut_res[3]], writes=[UTD[1].r])
    run(px_tile(UTD[1], 0, 15))
    load_ut(UTD[0], 4)
    run(c1_front(0))
    for ob in range(4):
        run(ilv(c1_pool(ob), c1_front(ob + 1) if ob < 3 else None, p0_block(5 + ob, UTP) if ob < 3 else None))
    run(pool_tile(31, True))
    S.barrier()
    ph.close()
    if stop_after == "C1":
        return nc, S

    wo_stack = ExitStack()
    WO = sb(wo_stack, "w_out", [128, 16, D], BF16)
    ph = ExitStack()
    set_psum(tp=0, mm=4, scan=True, sa=0)
    UTD = [sb(ph, f"ut{i}", [128, 8, 512], BF16) for i in range(2)]
    MXT = [sb(ph, f"mxt{i}", [128, 8, 516], BF16) for i in range(2)]
    CTB = sb(ph, "ctb", [128, 8, 512], BF16)
    SCB = sb(ph, "scb", [128, 8, 512], BF16)
    QTB = [sb(ph, f"qtb{i}", [128, 8, 512], BF16) for i in range(2)]
    KTB = [sb(ph, f"ktb{i}", [128, 8, 512], BF16) for i in range(2)]
    NKV = 6
    KT_ = [sb(ph, f"kt{i}", [128, D], BF16) for i in range(NKV)]
    VX = [sb(ph, f"vx{i}", [128, 4, 257], BF16) for i in range(NKV)]
    HF = [sb(ph, f"hf{i}", [128, D], F32) for i in range(2)]
    ZGS = [sb(ph, f"zg{i}", [128, 32], F32) for i in range(4)]
    ST = [sb(ph, f"st{i}", [128, 128], BF16) for i in range(4)]
    KS = [sb(ph, f"ks{i}", [128, 256], BF16) for i in range(2)]
    DN = [sb(ph, f"dn{i}", [128, 4], F32) for i in range(4)]
    CST_ = [sb(ph, f"cstate{h}", [128, 2, 257], F32) for h in range(4)]
    CB = [sb(ph, f"cb{h}", [128, 2, 257], BF16) for h in range(4)]

    def init_state(CST_, CB):
        for h in range(4):
            CST_[h].sub(0), CST_[h].sub(1)
            S.op("pool", lambda e: e.memset(CST_[h].t[:, :, :], 0.0), writes=CST_[h].all())
            S.op("pool", lambda e: e.memset(CB[h].t[:, :, :], 0.0), writes=[CB[h].r])

    init_state(CST_, CB)
    for i in range(NKV):
        S.op("pool", lambda e: e.memset(VX[i].t[:, :, 256:257], 1.0), writes=[VX[i].r])
    S.op("pool", lambda e: e.memset(MXT[0].t[:, :, 0:2], 0.0), writes=[MXT[0].r])

    def gate_tile(T, CTt, ct_r, MXt, mx_r):
        ZG = ZGS[T % 4]
        GP = next_mm()
        for cc in range(8):
            S.op("pe", lambda e: e.matmul(GP.t[:, 0:16], lhsT=CTt(cc), rhs=WF.t[:, cc, :], start=(cc == 0), stop=False),
                 reads=[ct_r, WF.r], writes=[GP.r])
        for cc in range(8):
            S.op("pe", lambda e: e.matmul(GP.t[:, 0:16], lhsT=MXt(cc), rhs=WF.t[:, 8 + cc, :], start=False, stop=(cc == 7)),
                 reads=[mx_r, WF.r], writes=[GP.r])
        S.op("dve", lambda e: e.tensor_tensor(out=ZG.t[:, 0:16], in0=GP.t[:, 0:16], in1=GBIAS, op=ALU.add),
             reads=[GP.r, BC.r], writes=[ZG.r])
        S.op("act", lambda e: e.activation(out=ZG.t[:, 24:32], in_=ZG.t[:, 8:16], func=AF.Exp, scale=-1.0),
             reads=[ZG.r], writes=[ZG.r])
        S.op("dve", lambda e: e.tensor_scalar_add(out=ZG.t[:, 24:32], in0=ZG.t[:, 24:32], scalar1=1.0),
             reads=[ZG.r], writes=[ZG.r])
        S.op("act", lambda e: e.activation(out=ZG.t[:, 16:24], in_=ZG.t[:, 24:32], func=AF.Ln), reads=[ZG.r], writes=[ZG.r])
        yield

    def gate_tile_b(T):
        ZG = ZGS[T % 4]
        g0 = T * 32
        GP = next_mm()
        S.op("pe", lambda e: e.matmul(GP.t[:, 16:20], lhsT=LF, rhs=ZG.t[:, 16:20], start=True, stop=True),
             reads=[CST.r, ZG.r], writes=[GP.r])
        S.op("pe", lambda e: e.matmul(GP.t[:, 20:24], lhsT=LR, rhs=ZG.t[:, 20:24], start=True, stop=True),
             reads=[CST.r, ZG.r], writes=[GP.r])
        S.op("pe", lambda e: e.matmul(GP.t[:, 24:32], lhsT=ONES, rhs=ZG.t[:, 16:24], start=True, stop=True),
             reads=[CST.r, ZG.r], writes=[GP.r])
        gr = gs_res[T]
        S.op("dve", lambda e: e.tensor_tensor(out=ZG.t[:, 24:32], in0=ZG.t[:, 0:8], in1=GP.t[:, 16:24], op=ALU.add),
             reads=[ZG.r, GP.r], writes=[ZG.r])
        S.op("act", lambda e: e.activation(out=GS.t[:, g0:g0 + 8], in_=ZG.t[:, 24:32], func=AF.Exp), reads=[ZG.r], writes=[gr["as"]])
        S.op("act", lambda e: e.activation(out=GS.t[:, g0 + 16:g0 + 24], in_=GP.t[:, 16:24], func=AF.Exp), reads=[GP.r], writes=[gr["fl"]])
        S.op("act", lambda e: e.activation(out=GS.t[:, g0 + 24:g0 + 32], in_=GP.t[:, 24:32], func=AF.Exp, scale=-1.0),
             reads=[GP.r], writes=[gr["dec"]])
        S.op("dve", lambda e: e.tensor_tensor(out=GS.t[:, g0 + 8:g0 + 16], in0=GS.t[:, g0:g0 + 8], in1=GS.t[:, g0 + 24:g0 + 32],
                                              op=ALU.mult), reads=[gr["as"], gr["dec"]], writes=[gr["ae"]])
        yield

    SCAN_ON_DVE = [False]

    def scan_step(bufs, T, d, h, QTt, q_r, KTt, k_r, ktok, vx, want_out, out_cb):
        ST, KS, DN, CST_, CB = bufs
        g0 = T * 32
        gr = gs_res[T]
        col = d * 4 + h
        a_s = GS.t[:, g0 + col:g0 + col + 1]
        a_e = GS.t[:, g0 + 8 + col:g0 + 9 + col]
        fl = GS.t[:, g0 + 16 + col:g0 + 17 + col]
        dec = GS.t[:, g0 + 24 + col:g0 + 25 + col]
        mask = LF if d == 0 else LR
        bbank = SB[h % 2]
        abank = SA[h % len(SA)] if SA else bbank
        A = abank.t[:, 0:128]
        B = bbank.t[:, 128:385]
        st, ks, dn = ST[h], KS[h % len(KS)], DN[h]
        cst_, cb = CST_[h], CB[h]
        if SCAN_ON_DVE[0]:
            S.op("dve", lambda e: e.tensor_scalar(out=ks.t[:], in0=ktok.t[:, h * 256:(h + 1) * 256], scalar1=a_e, scalar2=None,
                                                  op0=ALU.mult), reads=[ktok.r, gr["ae"]], writes=[ks.r])
        else:
            S.op("act", lambda e: e.activation(out=ks.t[:], in_=ktok.t[:, h * 256:(h + 1) * 256], func=AF.Copy, scale=a_e),
                 reads=[ktok.r, gr["ae"]], writes=[ks.r])
        if want_out:
            for dc in range(2):
                S.op("pe", lambda e: e.matmul(A, lhsT=KTt(dc), rhs=QTt(dc), start=(dc == 0), stop=(dc == 1)),
                     reads=[k_r, q_r], writes=[abank.r])
            S.op("dve", lambda e: e.scalar_tensor_tensor(out=st.t[:], in0=A, scalar=a_s, in1=mask, op0=ALU.mult, op1=ALU.mult),
                 reads=[abank.r, gr["as"], CST.r], writes=[st.r])
            yield
            S.op("pe", lambda e: e.matmul(B, lhsT=st.t[:], rhs=vx.t[:, h, :], start=True, stop=False),
                 reads=[st.r, vx.r], writes=[bbank.r])
            for dc in range(2):
                S.op("pe", lambda e: e.matmul(B, lhsT=QTt(dc), rhs=cb.t[:, dc, :], start=False, stop=(dc == 1)),
                     reads=[q_r, cb.r], writes=[bbank.r])
            S.op("dve", lambda e: e.tensor_scalar(out=dn.t[:, 0:1], in0=B[:, 256:257], scalar1=fl, scalar2=None, op0=ALU.max),
                 reads=[bbank.r, gr["fl"]], writes=[dn.r])
        yield
        for dc in range(2):
            S.op("pe", lambda e: e.matmul(PD[dc].t[:, 0:257], lhsT=ks.t[:, dc * 128:(dc + 1) * 128], rhs=vx.t[:, h, :],
                                          start=True, stop=True), reads=[ks.r, vx.r], writes=[PD[dc].r])
        def upd(dc):
            S.op("dve", lambda e: e.scalar_tensor_tensor(out=cst_.t[:, dc, :], in0=cst_.t[:, dc, :], scalar=dec,
                                                          in1=PD[dc].t[:, 0:257], op0=ALU.mult, op1=ALU.add),
                 reads=[cst_.sub(dc), gr["dec"], PD[dc].r], writes=[cst_.sub(dc)])

        if want_out:
            S.op("dve", lambda e: e.scalar_tensor_tensor(out=dn.t[:, 1:2], in0=B[:, 256:257], scalar=-1.0, in1=dn.t[:, 0:1],
                                                          op0=ALU.mult, op1=ALU.max), reads=[bbank.r, dn.r], writes=[dn.r])
            upd(0)
            S.op("dve", lambda e: e.reciprocal(out=dn.t[:, 2:3], in_=dn.t[:, 1:2]), reads=[dn.r], writes=[dn.r])
            upd(1)
        else:
            upd(0)
            upd(1)
        if SCAN_ON_DVE[0]:
            S.op("dve", lambda e: e.tensor_copy(out=cb.t[:, :, :], in_=cst_.t[:, :, :]), reads=cst_.all(), writes=[cb.r])
        else:
            S.op("act", lambda e: e.activation(out=cb.t[:, :, :], in_=cst_.t[:, :, :], func=AF.Copy), reads=cst_.all(), writes=[cb.r])
        if want_out:
            out_cb(h, B, dn.t[:, 2:3], bbank.r, dn.r)
        yield

    def s1(b):
        UT = UTD[b % 2]
        if b + 1 < 8:
            load_ut(UTD[(b + 1) % 2], b + 1)
        yield
        mx = MXT[b % 2]
        yield from proj_fm(WMX, 0, 8, UT, 512, lambda cc, ps: S.op(
            "act", lambda e: e.activation(out=mx.t[:, cc, 2:514], in_=ps.t[:, 0:512], func=AF.Copy), reads=[ps.r], writes=[mx.r]),
            fine=False)
        if b > 0:
            pm = MXT[(b - 1) % 2]
            S.op("pool", lambda e: e.tensor_copy(out=pm.t[:, :, 514:516], in_=mx.t[:, :, 2:4]), reads=[mx.r], writes=[pm.r])
            S.op("pool", lambda e: e.tensor_copy(out=mx.t[:, :, 0:2], in_=pm.t[:, :, 512:514]), reads=[pm.r], writes=[mx.r])
        if b == 7:
            S.op("pool", lambda e: e.memset(mx.t[:, :, 514:516], 0.0), writes=[mx.r])

    def s2(b):
        own = b >= 4
        mx = MXT[b % 2]
        qtb, ktb = QTB[b % 2], KTB[b % 2]
        for cc in range(8):
            ps = next_mm()
            for j in range(5):
                S.op("pe", lambda e: e.matmul(ps.t[:, 0:512], lhsT=CD.t[:, cc * 5 + j, :], rhs=mx.t[:, cc, j:j + 512],
                                              start=(j == 0), stop=(j == 4)), reads=[CD.r, mx.r], writes=[ps.r])
            S.op("act", lambda e: e.activation(out=CTB.t[:, cc, :], in_=ps.t[:, 0:512], func=AF.Silu, bias=CONVB(cc)),
                 reads=[ps.r, VEC.r], writes=[CTB.r])
            if own:
                S.op("act", lambda e: e.activation(out=SCB.t[:, cc, :], in_=CTB.t[:, cc, :], func=AF.Copy, scale=SKIP(cc)),
                     reads=[CTB.r, VEC.r], writes=[SCB.r])
            yield
        if own:
            o0 = (b - 4) * 512
            S.dma("sp", SC_s[:, :, o0:o0 + 512], SCB.t[:, :, :], reads=[SCB.r], writes=[r_scr["SC"]])
            for cc in range(8):
                ps = next_mm()
                S.op("pe", lambda e: e.matmul(ps.t[:, 0:512], lhsT=BD.t[:, cc, :], rhs=CTB.t[:, cc, :], start=True, stop=True),
                     reads=[BD.r, CTB.r], writes=[ps.r])
                S.op("act", lambda e: e.activation(out=qtb.t[:, cc, :], in_=ps.t[:, 0:512], func=AF.Copy), reads=[ps.r], writes=[qtb.r])
                ps = next_mm()
                S.op("pe", lambda e: e.matmul(ps.t[:, 0:512], lhsT=BD.t[:, 8 + cc, :], rhs=CTB.t[:, cc, :], start=True, stop=True),
                     reads=[BD.r, CTB.r], writes=[ps.r])
                S.op("act", lambda e: e.activation(out=ktb.t[:, cc, :], in_=ps.t[:, 0:512], func=AF.Copy, scale=1.0 / 16),
                     reads=[ps.r], writes=[ktb.r])
                yield
            S.dma("sp", QT_s[:, :, o0:o0 + 512], qtb.t[:, :, :], reads=[qtb.r], writes=[r_scr["QT"]])
            S.dma("sp", KT_s[:, :, o0:o0 + 512], ktb.t[:, :, :], reads=[ktb.r], writes=[r_scr["KT"]])
        for i in range(4):
            T = b * 4 + i
            kt, vx = KT_[T % NKV], VX[T % NKV]
            for half in range(2):
                ps = next_mm()
                for c4 in range(4):
                    cc = half * 4 + c4
                    S.op("pe", lambda e: e.matmul(ps.t[:, c4 * 128:(c4 + 1) * 128], lhsT=CTB.t[:, cc, i * 128:(i + 1) * 128],
                                                  rhs=BD.t[:, 8 + cc, :], start=True, stop=True),
                         reads=[CTB.r, BD.r], writes=[ps.r])
                S.op("act", lambda e: e.activation(out=kt.t[:, half * 512:(half + 1) * 512], in_=ps.t[:, 0:512], func=AF.Copy,
                                                   scale=1.0 / 16), reads=[ps.r], writes=[kt.r])
                ps = next_mm()
                for c4 in range(4):
                    cc = half * 4 + c4
                    S.op("pe", lambda e: e.matmul(ps.t[:, c4 * 128:(c4 + 1) * 128], lhsT=mx.t[:, cc, 2 + i * 128:2 + (i + 1) * 128],
                                                  rhs=BD.t[:, 16 + cc, :], start=True, stop=True),
                         reads=[mx.r, BD.r], writes=[ps.r])
                S.op("act", lambda e: e.activation(out=vx.t[:, half * 2:(half + 1) * 2, 0:256],
                                                   in_=ps.t[:, 0:512].rearrange("p (h e) -> p h e", e=256), func=AF.Copy),
                     reads=[ps.r], writes=[vx.r])
                yield
            yield from gate_tile(T, lambda cc: CTB.t[:, cc, i * 128:(i + 1) * 128], CTB.r,
                                 lambda cc: mx.t[:, cc, 2 + i * 128:2 + (i + 1) * 128], mx.r)
            if own:
                S.dma("sp", K_s[T - 16, :, :], kt.t[:], reads=[kt.r], writes=[r_scr["K"]])
                S.dma("sp", V_s[T - 16, :, :].rearrange("p (h e) -> p h e", e=256), vx.t[:, :, 0:256], reads=[vx.r],
                      writes=[r_scr["V"]])
        for i in range(4):
            yield from gate_tile_b(b * 4 + i)

    def fscan(b):
        own = b >= 4
        qtb, ktb = QTB[b % 2], KTB[b % 2]
        bufs = (ST, KS, DN, CST_, CB)
        for i in range(4):
            T = b * 4 + i
            kt, vx = KT_[T % NKV], VX[T % NKV]
            hf = HF[T % 2]

            def out_cb(h, B, r, b_r, dn_r, hf=hf):
                if SCAN_ON_DVE[0]:
                    S.op("dve", lambda e: e.tensor_scalar(out=hf.t[:, h * 256:(h + 1) * 256], in0=B[:, 0:256], scalar1=r,
                                                          scalar2=None, op0=ALU.mult), reads=[b_r, dn_r], writes=[hf.sub(h)])
                else:
                    S.op("act", lambda e: e.activation(out=hf.t[:, h * 256:(h + 1) * 256], in_=B[:, 0:256], func=AF.Copy,
                                                       scale=r), reads=[b_r, dn_r], writes=[hf.sub(h)])

            def step(h, i=i, T=T, kt=kt, vx=vx, out_cb=out_cb):
                return scan_step(bufs, T, 0, h,
                                 lambda dc: qtb.t[:, 2 * h + dc, i * 128:(i + 1) * 128], qtb.r,
                                 lambda dc: ktb.t[:, 2 * h + dc, i * 128:(i + 1) * 128], ktb.r,
                                 kt, vx, own, out_cb)

            if b == 7:
                SCAN_ON_DVE[0] = False
            yield from ilv(step(0), step(1))
            yield from ilv(step(2), step(3))
            if own:
                S.dma("sp", HF_s[T - 16, :, :], hf.t[:], reads=hf.all(), writes=[r_scr["HF"]])

    SCAN_ON_DVE[0] = True
    load_ut(UTD[0], 0)
    run(s1(0))
    run(s1(1))
    run(s2(0))
    for b in range(8):
        if b == 4:
            S.dma("pool", WO.t[:, :, :], w_out.rearrange("(k p) d -> p k d", p=128), writes=[WO.r])
        run(ilv(fscan(b), seq(s1(b + 2) if b + 2 < 8 else None, s2(b + 1) if b + 1 < 8 else None)))
    if DEBUG:
        S.dma("sp", GS_s[:, :], GS.t[:], reads=[r for g in gs_res for r in g.values()], writes=[r_scr["GS"]])
    S.barrier()
    ph.close()
    SCAN_ON_DVE[0] = False
    if stop_after == "AB":
        return nc, S

    ph = ExitStack()
    set_psum(tp=1, mm=3, scan=True, sa=0)
    NS, NA, NB = 3, 3, 4
    XT = [sb(ph, f"xt{i}", [128, D], F32) for i in range(NB)]
    QT2 = [sb(ph, f"qt{i}", [128, 8, 128], BF16) for i in range(NS)]
    KT2 = [sb(ph, f"ktt{i}", [128, 8, 128], BF16) for i in range(NS)]
    SC2 = [sb(ph, f"sc{i}", [128, 8, 128], BF16) for i in range(NA)]
    SMZ2 = [sb(ph, f"smz{i}", [128, 8, 128], BF16) for i in range(NA)]
    YP2 = [sb(ph, f"yp{i}", [128, 8, 128], BF16) for i in range(NB)]
    K2 = [sb(ph, f"k{i}", [128, D], BF16) for i in range(NS)]
    VX2 = [sb(ph, f"vx{i}", [128, 4, 257], BF16) for i in range(NS)]
    HF2 = [sb(ph, f"hf{i}", [128, D], F32) for i in range(NS)]
    HH = [sb(ph, f"hh{i}", [128, D], F32) for i in range(2)]
    HN = sb(ph, "hn", [128, D], BF16)
    T2 = sb(ph, "t2", [128, 8, 128], F32)
    YM = [sb(ph, f"ym{i}", [128, 8, 128], BF16) for i in range(2)]
    OO = sb(ph, "oo", [128, D], F32)
    OS = sb(ph, "os", [128, D], F32)
    P1 = sb(ph, "p1", [128, D], F32)
    OUTT = [sb(ph, f"outt{i}", [128, D], F32) for i in range(2)]
    BS = sb(ph, "bs", [128, 4, 6], F32)
    MV = sb(ph, "mv", [128, 4, 2], F32)
    RS = sb(ph, "rs", [128, 12], F32)
    ST = [sb(ph, f"st{i}", [128, 128], BF16) for i in range(4)]
    KS = [sb(ph, f"ks{i}", [128, 256], BF16) for i in range(4)]
    DN = [sb(ph, f"dn{i}", [128, 4], F32) for i in range(4)]
    CST_ = [sb(ph, f"cstate{h}", [128, 2, 257], F32) for h in range(4)]
    CB = [sb(ph, f"cb{h}", [128, 2, 257], BF16) for h in range(4)]

    init_state(CST_, CB)
    for i in range(NS):
        S.op("pool", lambda e: e.memset(VX2[i].t[:, :, 256:257], 1.0), writes=[VX2[i].r])

    def load_tile(T):
        s = T % NS
        o0 = (T - 16) * 128
        S.dma("sp", QT2[s].t[:, :, :], QT_s[:, :, o0:o0 + 128], reads=[r_scr["QT"]], writes=[QT2[s].r])
        S.dma("sp", KT2[s].t[:, :, :], KT_s[:, :, o0:o0 + 128], reads=[r_scr["KT"]], writes=[KT2[s].r])
        S.dma("sp", K2[s].t[:], K_s[T - 16, :, :], reads=[r_scr["K"]], writes=[K2[s].r])
        S.dma("sp", VX2[s].t[:, :, 0:256], V_s[T - 16, :, :].rearrange("p (h e) -> p h e", e=256), reads=[r_scr["V"]],
              writes=[VX2[s].r])
        S.dma("sp", HF2[s].t[:], HF_s[T - 16, :, :], reads=[r_scr["HF"]], writes=HF2[s].all())
        a, b = T % NA, T % NB
        S.dma("sp", SC2[a].t[:, :, :], SC_s[:, :, o0:o0 + 128], reads=[r_scr["SC"]], writes=[SC2[a].r])
        S.dma("sp", SMZ2[a].t[:, :, :], SMZ_s[:, :, o0:o0 + 128], reads=[r_scr["SMZ"]], writes=[SMZ2[a].r])
        S.dma("sp", YP2[b].t[:, :, :], YP_s[:, :, o0:o0 + 128], reads=[r_scr["YP"]], writes=[YP2[b].r])
        S.dma("sp", XT[b].t[:], xl[T * 128:(T + 1) * 128, :], writes=[XT[b].r])

    def rscan(T):
        s = T % NS
        qt, ktt, kk, vx, hf, hh = QT2[s], KT2[s], K2[s], VX2[s], HF2[s], HH[T % 2]
        bufs = (ST, KS, DN, CST_, CB)

        def out_cb(h, B, r, b_r, dn_r):
            S.op("dve", lambda e: e.scalar_tensor_tensor(out=hh.t[:, h * 256:(h + 1) * 256], in0=B[:, 0:256], scalar=r,
                                                          in1=hf.t[:, h * 256:(h + 1) * 256], op0=ALU.mult, op1=ALU.add),
                 reads=[b_r, dn_r, hf.sub(h)], writes=[hh.sub(h)])

        def step(h):
            return scan_step(bufs, T, 1, h, lambda dc: qt.t[:, 2 * h + dc, :], qt.r, lambda dc: ktt.t[:, 2 * h + dc, :], ktt.r,
                             kk, vx, True, out_cb)

        yield from ilv(step(0), step(1))
        yield from ilv(step(2), step(3))

    def final_a(T):
        sc, smz, hh, ym = SC2[T % NA], SMZ2[T % NA], HH[T % 2], YM[T % 2]
        for h in range(4):
            S.op("dve", lambda e: e.bn_stats(out=BS.t[:, h, :], in_=hh.t[:, h * 256:(h + 1) * 256]), reads=[hh.sub(h)], writes=[BS.sub(h)])
            if h % 2 == 1:
                yield
        for h in range(4):
            S.op("dve", lambda e: e.bn_aggr(out=MV.t[:, h, :], in_=BS.t[:, h, :]), reads=[BS.sub(h)], writes=[MV.sub(h)])
        S.op("pool", lambda e: e.tensor_scalar_add(out=RS.t[:, 0:4], in0=MV.t[:, :, 1], scalar1=EPS), reads=MV.all(), writes=[RS.r])
        S.op("pool", lambda e: e.tensor_tensor(out=RS.t[:, 4:8], in0=RS.t[:, 0:4], in1=MH.t[:, 0:4], op=ALU.pow),
             reads=[RS.r, MH.r], writes=[RS.r])
        S.op("dve", lambda e: e.scalar_tensor_tensor(out=RS.t[:, 8:12], in0=MV.t[:, :, 0], scalar=-1.0, in1=RS.t[:, 4:8],
                                                      op0=ALU.mult, op1=ALU.mult), reads=MV.all() + [RS.r], writes=[RS.r])
        yield
        for h in range(4):
            S.op("act", lambda e: e.activation(out=HN.t[:, h * 256:(h + 1) * 256], in_=hh.t[:, h * 256:(h + 1) * 256],
                                               func=AF.Identity, scale=RS.t[:, 4 + h:5 + h], bias=RS.t[:, 8 + h:9 + h]),
                 reads=[hh.sub(h), RS.r], writes=[HN.sub(h)])
            if h % 2 == 1:
                yield
        tp = next_tp()
        for cc in range(8):
            S.op("pe", lambda e: e.transpose(tp.t[:, cc, :], HN.t[:, cc * 128:(cc + 1) * 128], IDB.t[:]),
                 reads=[HN.sub(cc // 2), IDB.r], writes=[tp.r])
        for cc in range(8):
            S.op("dve", lambda e: e.scalar_tensor_tensor(out=T2.t[:, cc, :], in0=tp.t[:, cc, :], scalar=WN(cc), in1=sc.t[:, cc, :],
                                                          op0=ALU.mult, op1=ALU.add), reads=[tp.r, VEC.r, sc.r], writes=[T2.sub(cc)])
        S.op("pool", lambda e: e.tensor_tensor(out=ym.t[:, :, :], in0=T2.t[:, :, :], in1=smz.t[:, :, :], op=ALU.mult),
             reads=T2.all() + [smz.r], writes=[ym.r])
        yield

    def final_b(T):
        yp, xt, ym = YP2[T % NB], XT[T % NB], YM[T % 2]
        for half in range(2):
            ps = next_mm()
            for k in range(16):
                lhs = yp.t[:, k, :] if k < 8 else ym.t[:, k - 8, :]
                S.op("pe", lambda e: e.matmul(ps.t[:, 0:512], lhsT=lhs, rhs=WO.t[:, k, half * 512:(half + 1) * 512],
                                              start=(k == 0), stop=(k == 15)), reads=[yp.r, ym.r, WO.r], writes=[ps.r])
                if k % 2 == 1 and k < 15:
                    yield
            S.op("act", lambda e: e.activation(out=P1.t[:, half * 512:(half + 1) * 512], in_=ps.t[:, 0:512], func=AF.Copy),
                 reads=[ps.r], writes=[P1.sub(half)])
            yield
        OO.sub(0), OO.sub(1)
        S.op("pool", lambda e: e.tensor_tensor(out=OO.t[:], in0=P1.t[:], in1=xt.t[:], op=ALU.add),
             reads=P1.all() + [xt.r], writes=OO.all())
        rms_scale(OO, 3)
        ot = OUTT[T % 2]
        S.op("act", lambda e: e.activation(out=OS.t[:], in_=OO.t[:], func=AF.Copy, scale=SS.t[:, 5:6]),
             reads=OO.all() + [SS.r], writes=[OS.r])
        S.op("pool", lambda e: e.tensor_tensor(out=ot.t[:], in0=OS.t[:], in1=G_OUT, op=ALU.mult),
             reads=[OS.r, BC.r], writes=[ot.r])
        S.dma("sp", out[(T - 16) * 128:(T - 15) * 128, :], ot.t[:], reads=[ot.r], writes=[r_scr["out"]])
        yield

    load_tile(31)
    load_tile(30)
    for T in range(31, 13, -1):
        if 16 < T <= 30:
            load_tile(T - 1)
        run(ilvw((rscan(T) if T >= 16 else None, 1),
                 (final_a(T + 1) if 16 <= T + 1 <= 31 else None, 1),
                 (final_b(T + 2) if 16 <= T + 2 <= 31 else None, 2)))
    S.barrier()
    ph.close()
    wo_stack.close()
    abw.close()
    return nc, S


def _block_diag(w):
    o = np.zeros((8, 128, 128), np.float32)
    for cc in range(8):
        for n in range(32):
            o[cc, 4 * n:4 * n + 4, 4 * n:4 * n + 4] = w[32 * cc + n]
    return o


def _bands(flip):
    o = np.zeros((4, 4, 128, 128), np.float32)
    for g, w in enumerate(POOL_WINDOWS):
        left = (w - 1) // 2
        right = w - 1 - left
        if flip:
            left, right = right, left
        for kind, rel, end in ((0, -1, False), (1, 0, False), (2, 1, False), (3, 0, True)):
            for t in range(128):
                lo, hi = t - left, t + right
                if end:
                    hi = min(hi, 127)
                cntv = hi - lo + 1
                for sl in range(128):
                    s = rel * 128 + sl
                    v = 0.0
                    if lo <= s <= hi:
                        v = 1.0 / cntv
                    if rel == 0 and s == t:
                        v -= 1.0
                    o[g, kind, sl, t] = v
    return np.ascontiguousarray(np.transpose(o, (2, 0, 1, 3)).reshape(128, 16 * 128))


def _pvec(v):
    return np.ascontiguousarray(np.asarray(v, np.float32).reshape(8, 128).T)


_CACHE = {}


def _prep_inputs(x, norm_in_g, w_in, pool_w, pool_scale, conv_w, conv_b, w_q, w_k, w_v, w_gates, b_gates,
                 mh_norm_w, skip_w, w_out, norm_out_g):
    f = lambda a: np.ascontiguousarray(np.asarray(a, np.float32))
    x = f(x)
    w_in0, w_out0, pool_w0 = f(w_in)[0], f(w_out)[0], f(pool_w)[0].reshape(1024, 256)
    conv_w0, w_g, b_g = f(conv_w)[0], f(w_gates)[0], f(b_gates)[0]
    bdq, bdk, bdv = _block_diag(f(w_q)[0]), _block_diag(f(w_k)[0]), _block_diag(f(w_v)[0])
    bd = np.concatenate([bdq, bdk, bdv], 0)
    bd_h = np.ascontiguousarray(np.transpose(bd, (1, 0, 2)).reshape(128, 24 * 128))
    bdT_h = np.ascontiguousarray(np.transpose(bd, (2, 0, 1)).reshape(128, 24 * 128))
    vecs = np.concatenate([_pvec(f(pool_scale)[0]), _pvec(f(conv_b)[0]), _pvec(f(mh_norm_w)[0]), _pvec(f(skip_w)[0])], 1)
    tri = np.tril(np.ones((128, 128), np.float32))
    LFm, LRm = np.ascontiguousarray(tri.T), tri
    cst = np.ascontiguousarray(np.concatenate([LFm, LRm, np.ones((128, 128), np.float32)], 1))
    ident = np.eye(128, dtype=np.float32)
    maps = []
    for core in range(8):
        b, half = core // 2, core % 2
        flip = half == 0
        dF = 1 if flip else 0
        dR = 1 - dF
        xl = x[b][::-1] if flip else x[b]
        cw = conv_w0[::-1] if flip else conv_w0
        cd = np.zeros((128, 40, 128), np.float32)
        idx = np.arange(128)
        for cc in range(8):
            for j in range(5):
                cd[idx, cc * 5 + j, idx] = cw[j, cc * 128:(cc + 1) * 128]
        wg16 = np.concatenate([w_g[dF][:, 0:4], w_g[dR][:, 0:4], w_g[dF][:, 4:8], w_g[dR][:, 4:8]], 1)
        wg_h = np.ascontiguousarray(np.transpose(wg16.reshape(24, 128, 16), (1, 0, 2)).reshape(128, 24 * 16))
        bias16 = np.concatenate([b_g[dF][0:4], b_g[dR][0:4], b_g[dF][4:8], b_g[dR][4:8]])
        bcast = np.concatenate([np.tile(f(norm_in_g)[0][None, :], (128, 1)), np.tile(f(norm_out_g)[None, :], (128, 1)),
                                np.tile(bias16[None, :], (128, 1))], 1)
        maps.append({
            "xl": np.ascontiguousarray(xl), "w_in": w_in0, "w_out": w_out0, "pool_w": pool_w0,
            "cdiag": np.ascontiguousarray(cd.reshape(128, 40 * 128)), "bd": bd_h, "bdT": bdT_h, "wg": wg_h,
            "vecs": np.ascontiguousarray(vecs), "bcast": np.ascontiguousarray(bcast.astype(np.float32)), "cst": cst,
            "ident": ident, "bands": _bands(flip),
        })
    return maps


def kernel(**inputs):
    if "nc" not in _CACHE:
        _CACHE["nc"] = build_program()[0]
    nc = _CACHE["nc"]
    maps = _prep_inputs(**inputs)
    res = run_bass_kernel_spmd(nc, maps, core_ids=list(range(8)))
    outp = np.empty((4, SEQ, D), np.float32)
    for core in range(8):
        b, half = core // 2, core % 2
        y = res.results[core]["out"]
        if half == 0:
            outp[b, 0:NOWN] = y[::-1]
        else:
            outp[b, NOWN:SEQ] = y
    if DEBUG:
        _CACHE["last"] = res.results
    return outp
```

```python
import numpy as np
from contextlib import ExitStack
import concourse.bass as bass
import concourse.mybir as mybir
from concourse.bass_utils import run_bass_kernel_spmd

F32 = mybir.dt.float32
BF16 = mybir.dt.bfloat16
AF = mybir.ActivationFunctionType
ALU = mybir.AluOpType

D = 1024
SEQ = 4096
NOWN = 2048
EPS = 1e-6
POOL_WINDOWS = (2, 4, 8, 16)
DEBUG = False
OP_LIMIT = None
FILL_EVERY = 0


class Res:
    __slots__ = ("last_w", "readers", "name")

    def __init__(self, name=""):
        self.last_w = None
        self.readers = {}
        self.name = name


class Sched:
    SAME_ENGINE_SYNC = True

    def __init__(self, nc, ndma=24):
        self.nc = nc
        self.engs = {"pe": nc.tensor, "act": nc.scalar, "dve": nc.vector, "pool": nc.gpsimd, "sp": nc.sync}
        self.sem = {k: nc.alloc_semaphore(name=f"s_{k}") for k in self.engs}
        self.count = {k: 0 for k in self.engs}
        self.waited = {k: {} for k in self.engs}
        self.snaps = {k: [] for k in self.engs}
        self.dma_snap = {}
        self.TRANSITIVE = True
        self.dma_sems = [nc.alloc_semaphore(name=f"s_dma{i}") for i in range(ndma)]
        self.dma_cnt = [0] * ndma
        self.n_sw = 6
        self.dma_next = {"sw": 0, "hw": 0}
        self.nwaits = 0
        self.nops = 0
        self.limit = None
        self.log = []

    def _wait(self, e, tok, raw=False):
        kind, x, c = tok
        key = (kind, x)
        if kind == "eng" and x == e and (e == "pe" or not (self.SAME_ENGINE_SYNC or raw)):
            return
        if self.waited[e].get(key, 0) >= c:
            return
        sem = self.sem[x] if kind == "eng" else self.dma_sems[x]
        self.engs[e].wait_ge(sem, c)
        self.nwaits += 1
        self.waited[e][key] = c
        if self.TRANSITIVE:
            if kind == "eng":
                snap = None
                for cnt, kn in reversed(self.snaps[x]):
                    if cnt <= c:
                        snap = kn
                        break
            else:
                snap = self.dma_snap.get((x, c))
            if snap:
                w = self.waited[e]
                for k2, v2 in snap.items():
                    if w.get(k2, 0) < v2:
                        w[k2] = v2

    def _deps(self, e, reads, writes):
        for r in reads:
            if r.last_w is not None:
                self._wait(e, r.last_w, raw=True)
        for w in writes:
            if w.last_w is not None:
                self._wait(e, w.last_w)
            for d in w.readers.values():
                self._wait(e, d)

    def _commit(self, tok, reads, writes):
        for r in reads:
            r.readers[(tok[0], tok[1])] = tok
        for w in writes:
            w.last_w = tok
            w.readers = {}

    def op(self, e, emit, reads=(), writes=()):
        if self.limit is not None and self.nops >= self.limit:
            return None
        self._deps(e, reads, writes)
        inst = emit(self.engs[e])
        self.count[e] += 1
        self.nops += 1
        if DEBUG:
            import sys
            self.log.append((self.nops, e, sys._getframe(1).f_lineno))
        inst.then_inc(self.sem[e], 1)
        if self.TRANSITIVE and (not self.snaps[e] or self.snaps[e][-1][1] != self.waited[e]):
            self.snaps[e].append((self.count[e], dict(self.waited[e])))
        tok = ("eng", e, self.count[e])
        self._commit(tok, reads, writes)
        return tok

    def dma(self, e, out, in_, reads=(), writes=(), **kw):
        if self.limit is not None and self.nops >= self.limit:
            return None
        self.nops += 1
        if DEBUG:
            import sys
            self.log.append((self.nops, "dma-" + e, sys._getframe(1).f_lineno))
        if e == "pool":
            kw.setdefault("max_dma_last_dim", 4096)
        self._deps(e, reads, writes)
        if e == "pool":
            k = self.dma_next["sw"]
            self.dma_next["sw"] = (k + 1) % self.n_sw
        else:
            k = self.n_sw + self.dma_next["hw"]
            self.dma_next["hw"] = (self.dma_next["hw"] + 1) % (len(self.dma_sems) - self.n_sw)
        if self.dma_cnt[k] > 0:
            self._wait(e, ("dma", k, 16 * self.dma_cnt[k]))
        inst = self.engs[e].dma_start(out=out, in_=in_, **kw)
        self.dma_cnt[k] += 1
        inst.then_inc(self.dma_sems[k], 16)
        if self.TRANSITIVE:
            self.dma_snap[(k, 16 * self.dma_cnt[k])] = dict(self.waited[e])
        tok = ("dma", k, 16 * self.dma_cnt[k])
        self._commit(tok, reads, writes)
        return tok

    def barrier(self):
        for e in self.engs:
            for x in self.engs:
                if x != e and self.count[x] > 0:
                    self._wait(e, ("eng", x, self.count[x]))
            for k in range(len(self.dma_sems)):
                if self.dma_cnt[k] > 0:
                    self._wait(e, ("dma", k, 16 * self.dma_cnt[k]))

    def finish(self, e, resources):
        for r in resources:
            if r.last_w is not None:
                self._wait(e, r.last_w)


class Tl:
    def __init__(self, t, name=""):
        self.t = t
        self.r = Res(name)
        self.subs = {}

    def sub(self, k):
        if k not in self.subs:
            r = Res(f"{self.r.name}.{k}")
            r.last_w = self.r.last_w
            r.readers = dict(self.r.readers)
            self.subs[k] = r
        return self.subs[k]

    def all(self):
        return [self.r] + list(self.subs.values())


def build_program(stop_after=None):
    nc = bass.Bass("TRN2", target_bir_lowering=False)

    def din(name, shape, dt=F32):
        return nc.dram_tensor(name, shape, dt, kind="ExternalInput").ap()

    xl = din("xl", [SEQ, D])
    w_in = din("w_in", [D, 4 * D])
    w_out = din("w_out", [2 * D, D])
    pool_w = din("pool_w", [1024, 256])
    cdiag = din("cdiag", [128, 40 * 128])
    bd = din("bd", [128, 3 * 8 * 128])
    bdT = din("bdT", [128, 3 * 8 * 128])
    wg = din("wg", [128, 24 * 16])
    vecs = din("vecs", [128, 4 * 8])
    bcast = din("bcast", [128, 2 * D + 16])
    cst = din("cst", [128, 3 * 128])
    ident = din("ident", [128, 128])
    bands = din("bands", [128, 16 * 128])
    out = nc.dram_tensor("out", [NOWN, D], F32, kind="ExternalOutput").ap()

    skind = "ExternalOutput" if DEBUG else "Internal"

    def dscr(name, shape, dt):
        return nc.dram_tensor(name, shape, dt, kind=skind).ap()

    QT_s = dscr("QT_s", [128, 8, NOWN], BF16)
    KT_s = dscr("KT_s", [128, 8, NOWN], BF16)
    SC_s = dscr("SC_s", [128, 8, NOWN], BF16)
    SMZ_s = dscr("SMZ_s", [128, 8, NOWN], BF16)
    YP_s = dscr("YP_s", [128, 8, NOWN], BF16)
    UT_s = dscr("UT_s", [128, 8, SEQ], BF16)
    ut_res = [Res(f"ut_s{b}") for b in range(8)]
    K_s = dscr("K_s", [16, 128, D], BF16)
    V_s = dscr("V_s", [16, 128, D], BF16)
    HF_s = dscr("HF_s", [16, 128, D], F32)
    GS_s = dscr("GS_s", [128, 32 * 32], F32) if DEBUG else None
    r_scr = {n: Res(n) for n in ["QT", "KT", "SC", "SMZ", "YP", "K", "V", "HF", "out", "GS", "UT"]}

    S = Sched(nc)
    S.limit = OP_LIMIT
    glob = ExitStack()

    uniq = [0]

    def sb(stack, name, shape, dt):
        uniq[0] += 1
        return Tl(stack.enter_context(nc.sbuf_tensor(f"sb{uniq[0]}_{name}", shape, dt)), name)

    pstate = {"stack": None, "n": 0}
    TP, MM, SA, SB, PD = [], [], [], [], []

    def set_psum(tp, mm, scan, sa=0):
        if pstate["stack"] is not None:
            pstate["stack"].close()
        st = pstate["stack"] = ExitStack()

        def psum(name, shape, dt):
            pstate["n"] += 1
            return Tl(st.enter_context(nc.psum_tensor(f"ps{pstate['n']}_{name}", shape, dt)), name)

        TP[:] = [psum(f"tp{i}", [128, 8, 128], BF16) for i in range(tp)]
        MM[:] = [psum(f"mm{i}", [128, 512], F32) for i in range(mm)]
        SA[:] = [psum(f"sa{i}", [128, 512], F32) for i in range(sa)] if scan else []
        SB[:] = [psum(f"sb{i}", [128, 512], F32) for i in range(2)] if scan else []
        PD[:] = [psum(f"pd{i}", [128, 512], F32) for i in range(2)] if scan else []

    set_psum(tp=1, mm=7, scan=False)
    cnt = {"tp": 0, "mm": 0}

    def next_tp():
        cnt["tp"] += 1
        return TP[cnt["tp"] % len(TP)]

    def next_mm():
        cnt["mm"] += 1
        return MM[cnt["mm"] % len(MM)]

    fill = {"n": 0, "every": FILL_EVERY, "cnt": 0}

    def keep_warm():
        if fill["every"] <= 0 or (S.limit is not None and S.nops >= S.limit):
            return
        fill["cnt"] += 1
        if fill["cnt"] % fill["every"] == 0:
            pass
            fill["n"] += 1

    def ilv(*gens):
        active = [g for g in gens if g is not None]
        while active:
            for g in list(active):
                try:
                    next(g)
                    keep_warm()
                    yield
                except StopIteration:
                    active.remove(g)

    def ilvw(*pairs):
        active = [[g, w] for g, w in pairs if g is not None]
        while active:
            for item in list(active):
                g, w = item
                for _ in range(w):
                    try:
                        next(g)
                        yield
                    except StopIteration:
                        active.remove(item)
                        break

    def seq(*gens):
        for g in gens:
            if g is not None:
                yield from g

    def run(g):
        for _ in g:
            pass

    CST = sb(glob, "cst", [128, 3 * 128], F32)
    IDB = sb(glob, "idb", [128, 128], BF16)
    BC = sb(glob, "bc", [128, 2 * D + 16], F32)
    VEC = sb(glob, "vec", [128, 32], F32)
    GS = sb(glob, "gs", [128, 32 * 32], F32)
    MH = sb(glob, "mh", [128, 4], F32)
    SS = sb(glob, "ss", [128, 8], F32)
    JUNK = sb(glob, "junk", [128, D], BF16)
    gs_res = [{k: Res(f"gs{t}{k}") for k in ("as", "ae", "fl", "dec")} for t in range(32)]

    S.dma("sp", CST.t[:], cst[:, :], writes=[CST.r])
    S.dma("pool", IDB.t[:], ident[:, :], writes=[IDB.r])
    S.dma("sp", BC.t[:], bcast[:, :], writes=[BC.r])
    S.dma("sp", VEC.t[:], vecs[:, :], writes=[VEC.r])
    S.op("pool", lambda e: e.memset(MH.t[:], -0.5), writes=[MH.r])
    LF = CST.t[:, 0:128]
    LR = CST.t[:, 128:256]
    ONES = CST.t[:, 256:384]
    G_IN = BC.t[:, 0:D]
    G_OUT = BC.t[:, D:2 * D]
    GBIAS = BC.t[:, 2 * D:2 * D + 16]
    PSCALE = lambda cc: VEC.t[:, cc:cc + 1]
    CONVB = lambda cc: VEC.t[:, 8 + cc:9 + cc]
    WN = lambda cc: VEC.t[:, 16 + cc:17 + cc]
    SKIP = lambda cc: VEC.t[:, 24 + cc:25 + cc]

    w_in_v = w_in.rearrange("(kc p) c -> p kc c", p=128)

    def rms_scale(src, col):
        S.op("act", lambda e: e.activation(out=JUNK.t[:], in_=src.t[:], func=AF.Square, accum_out=SS.t[:, col:col + 1]),
             reads=src.all() + [SS.r], writes=[JUNK.r, SS.r])
        S.op("pool", lambda e: e.tensor_scalar(out=SS.t[:, col + 1:col + 2], in0=SS.t[:, col:col + 1], scalar1=1.0 / D,
                                                scalar2=EPS, op0=ALU.mult, op1=ALU.add), reads=[SS.r], writes=[SS.r])
        S.op("pool", lambda e: e.tensor_tensor(out=SS.t[:, col + 2:col + 3], in0=SS.t[:, col + 1:col + 2],
                                                in1=MH.t[:, 0:1], op=ALU.pow), reads=[SS.r, MH.r], writes=[SS.r])

    SSR = sb(glob, "ssr", [128, 16], F32)
    ssr_res = [Res(f"ssr{i}") for i in range(4)]

    class XPipe:
        def __init__(self, XT, XN, tiles, LA=2, LD=3):
            self.XT, self.XN, self.tiles, self.LA, self.LD = XT, XN, tiles, LA, LD
            self.i_a = self.i_d = 0
            assert len(XT) >= LD + 2

        def ahead_dma(self, upto):
            while self.i_d <= min(upto, len(self.tiles) - 1):
                idx = self.i_d
                xt = self.XT[idx % len(self.XT)]
                T = self.tiles[idx]
                S.dma("act", xt.t[:], xl[T * 128:(T + 1) * 128, :], writes=[xt.r])
                self.i_d += 1

        def ahead_stats(self, upto):
            while self.i_a <= min(upto, len(self.tiles) - 1):
                idx = self.i_a
                self.ahead_dma(idx)
                xt = self.XT[idx % len(self.XT)]
                c = (idx % 4) * 4
                r = ssr_res[idx % 4]
                S.op("act", lambda e: e.activation(out=JUNK.t[:], in_=xt.t[:], func=AF.Square, accum_out=SSR.t[:, c:c + 1]),
                     reads=[xt.r, r], writes=[JUNK.r, r])
                S.op("pool", lambda e: e.tensor_scalar(out=SSR.t[:, c + 1:c + 2], in0=SSR.t[:, c:c + 1], scalar1=1.0 / D,
                                                        scalar2=EPS, op0=ALU.mult, op1=ALU.add), reads=[r], writes=[r])
                S.op("pool", lambda e: e.tensor_tensor(out=SSR.t[:, c + 2:c + 3], in0=SSR.t[:, c + 1:c + 2],
                                                        in1=MH.t[:, 0:1], op=ALU.pow), reads=[r, MH.r], writes=[r])
                self.i_a += 1

        def prime(self):
            self.ahead_dma(self.LD)
            self.ahead_stats(self.LA)

        def get_xn(self, idx):
            self.ahead_stats(idx)
            xt = self.XT[idx % len(self.XT)]
            xn = self.XN[idx % len(self.XN)]
            c = (idx % 4) * 4
            S.op("dve", lambda e: e.scalar_tensor_tensor(out=xn.t[:], in0=xt.t[:], scalar=SSR.t[:, c + 2:c + 3], in1=G_IN,
                                                          op0=ALU.mult, op1=ALU.mult),
                 reads=[xt.r, ssr_res[idx % 4], BC.r], writes=[xn.r])
            return xn

        def after(self, idx):
            self.ahead_dma(idx + self.LD)
            self.ahead_stats(idx + self.LA)

    def make_uT(pipe, idx0, ntiles, UT):
        for i in range(ntiles):
            xn = pipe.get_xn(idx0 + i)
            yield
            tp = next_tp()
            for kc in range(8):
                S.op("pe", lambda e: e.transpose(tp.t[:, kc, :], xn.t[:, kc * 128:(kc + 1) * 128], IDB.t[:]),
                     reads=[xn.r, IDB.r], writes=[tp.r])
            if (idx0 + i) % 2 == 0:
                S.op("act", lambda e: e.activation(out=UT.t[:, :, i * 128:(i + 1) * 128], in_=tp.t[:, :, :], func=AF.Copy),
                     reads=[tp.r], writes=[UT.r])
            else:
                S.op("dve", lambda e: e.tensor_copy(out=UT.t[:, :, i * 128:(i + 1) * 128], in_=tp.t[:, :, :]),
                     reads=[tp.r], writes=[UT.r])
            pipe.after(idx0 + i)
            yield

    def proj_fm(W, wcol0, ncc, UT, ntok, evac, fine=False, w_r=None):
        for cc in range(ncc):
            ps = next_mm()
            for kc in range(8):
                S.op("pe", lambda e: e.matmul(ps.t[:, 0:ntok], lhsT=W.t[:, kc, wcol0 + cc * 128: wcol0 + (cc + 1) * 128],
                                              rhs=UT.t[:, kc, 0:ntok], start=(kc == 0), stop=(kc == 7)),
                     reads=[w_r or W.r, UT.r], writes=[ps.r])
                if fine and kc % 2 == 1 and kc < 7:
                    yield
            evac(cc, ps)
            yield

    abw = ExitStack()
    WMX = sb(abw, "w_mx", [128, 8, D], BF16)
    CD = sb(abw, "cd", [128, 40, 128], BF16)
    BD = sb(abw, "bd", [128, 24, 128], BF16)
    WF = sb(abw, "wf", [128, 16, 16], BF16)
    ph = ExitStack()
    WR = sb(ph, "w_rest", [128, 8, 3 * D], BF16)
    PW = sb(ph, "pool_w", [128, 8, 256], BF16)
    BND = sb(ph, "bands", [128, 16, 128], BF16)
    XT = [sb(ph, f"xt{i}", [128, D], F32) for i in range(5)]
    XN = [sb(ph, f"xn{i}", [128, D], BF16) for i in range(2)]
    UTD = [sb(ph, f"ut{i}", [128, 8, 512], BF16) for i in range(2)]
    p0_blocks = [3, 4, 5, 6, 7, 0, 1, 2]
    xpipe = XPipe(XT, XN, [b * 4 + i for b in p0_blocks for i in range(4)])
    xpipe.prime()
    NPX = 8
    PX = [sb(ph, f"px{i}", [128, D], BF16) for i in range(NPX)]
    SPZ = [sb(ph, f"spz{i}", [128, 8, 512], BF16) for i in range(2)]
    SMZ = [sb(ph, f"smz{i}", [128, 8, 512], BF16) for i in range(1)]
    PL = [sb(ph, f"pl{i}", [128, 8, 128], BF16) for i in range(2)]
    YPT = [sb(ph, f"ypt{i}", [128, 8, 128], BF16) for i in range(2)]


    def px_tile(UTb, i, T):
        pxt = PX[T % NPX]
        for cg in range(2):
            ps = next_mm()
            for kc in range(8):
                S.op("pe", lambda e: e.matmul(ps.t[:, 0:512], lhsT=UTb.t[:, kc, i * 128:(i + 1) * 128],
                                              rhs=WR.t[:, kc, cg * 512:(cg + 1) * 512], start=(kc == 0), stop=(kc == 7)),
                     reads=[UTb.r, WR.sub(0)], writes=[ps.r])
            S.op("dve", lambda e: e.tensor_copy(out=pxt.t[:, cg * 512:(cg + 1) * 512], in_=ps.t[:, 0:512]),
                 reads=[ps.r], writes=[pxt.r])
            yield

    def pool_tile(T, last):
        pl = PL[T % 2]
        ypt = YPT[T % 2]
        spz = SPZ[((T - 16) // 4) % 2]
        tcol = ((T - 16) % 4) * 128
        for half in range(2):
            ps = next_mm()
            for c4 in range(4):
                cc = half * 4 + c4
                g = cc // 2
                srcs = [(PX[(T - 1) % NPX], 0), (PX[T % NPX], 3 if last else 1)]
                if not last:
                    srcs.append((PX[(T + 1) % NPX], 2))
                for n, (pxs, kind) in enumerate(srcs):
                    S.op("pe", lambda e: e.matmul(ps.t[:, c4 * 128:(c4 + 1) * 128], lhsT=pxs.t[:, cc * 128:(cc + 1) * 128],
                                                  rhs=BND.t[:, g * 4 + kind, :], start=(n == 0), stop=(n == len(srcs) - 1)),
                         reads=[pxs.r, BND.r], writes=[ps.r])
            S.op("act", lambda e: e.activation(out=pl.t[:, half * 4:(half + 1) * 4, :],
                                               in_=ps.t[:, 0:512].rearrange("p (c t) -> p c t", t=128), func=AF.Copy),
                 reads=[ps.r], writes=[pl.r])
            yield
        for half in range(2):
            ps = next_mm()
            for c4 in range(4):
                oc = half * 4 + c4
                g = oc // 2
                for k2 in range(2):
                    S.op("pe", lambda e: e.matmul(ps.t[:, c4 * 128:(c4 + 1) * 128],
                                                  lhsT=PW.t[:, g * 2 + k2, (oc % 2) * 128:(oc % 2 + 1) * 128],
                                                  rhs=pl.t[:, g * 2 + k2, :], start=(k2 == 0), stop=(k2 == 1)),
                         reads=[PW.r, pl.r], writes=[ps.r])
            for c4 in range(4):
                oc = half * 4 + c4
                S.op("dve", lambda e: e.scalar_tensor_tensor(out=ypt.t[:, oc, :], in0=ps.t[:, c4 * 128:(c4 + 1) * 128],
                                                              scalar=PSCALE(oc), in1=spz.t[:, oc, tcol:tcol + 128],
                                                              op0=ALU.mult, op1=ALU.mult),
                     reads=[ps.r, VEC.r, spz.r], writes=[ypt.r])
            yield
        o0 = (T - 16) * 128
        S.dma("sp", YP_s[:, :, o0:o0 + 128], ypt.t[:, :, :], reads=[ypt.r], writes=[r_scr["YP"]])

    def load_ut(UTbuf, blk, queue="act"):
        S.dma(queue, UTbuf.t[:, :, :], UT_s[:, :, blk * 512:(blk + 1) * 512], reads=[ut_res[blk]], writes=[UTbuf.r])

    def c1_front(ob):
        UT = UTD[ob % 2]
        if ob < 3:
            load_ut(UTD[(ob + 1) % 2], 4 + ob + 1)
        yield
        spz, smz = SPZ[ob % 2], SMZ[0]
        for i in range(4):
            yield from px_tile(UT, i, 16 + ob * 4 + i)
        yield from proj_fm(WR, D, 8, UT, 512, lambda cc, ps: S.op(
            "act", lambda e: e.activation(out=spz.t[:, cc, :], in_=ps.t[:, 0:512], func=AF.Silu), reads=[ps.r], writes=[spz.r]),
            w_r=WR.sub(1))
        yield from proj_fm(WR, 2 * D, 8, UT, 512, lambda cc, ps: S.op(
            "act", lambda e: e.activation(out=smz.t[:, cc, :], in_=ps.t[:, 0:512], func=AF.Silu), reads=[ps.r], writes=[smz.r]),
            w_r=WR.sub(2))
        S.dma("sp", SMZ_s[:, :, ob * 512:(ob + 1) * 512], smz.t[:, :, :], reads=[smz.r], writes=[r_scr["SMZ"]])

    def c1_pool(ob):
        T0 = 16 + ob * 4
        for T in range(T0 - 1, T0 + 3):
            if T >= 16:
                yield from pool_tile(T, False)

    def p0_block(k, buf):
        blk = p0_blocks[k]
        yield from make_uT(xpipe, k * 4, 4, buf)
        S.dma("sp", UT_s[:, :, blk * 512:(blk + 1) * 512], buf.t[:, :, :], reads=[buf.r], writes=[ut_res[blk]])
        yield

    fold_tmp = ExitStack()
    BDT = sb(fold_tmp, "bdT", [128, 24, 128], F32)
    WG = sb(fold_tmp, "wg", [128, 24, 16], F32)
    S.dma("sp", BDT.t[:, :, :], bdT.rearrange("p (k q) -> p k q", q=128), writes=[BDT.r])
    S.dma("sp", WG.t[:, :, :], wg.rearrange("p (k q) -> p k q", q=16), writes=[WG.r])

    def fold_gen():
        GP = next_mm()
        for cc in range(8):
            S.op("pe", lambda e: e.matmul(GP.t[:, cc * 16:(cc + 1) * 16], lhsT=BDT.t[:, cc, :], rhs=WG.t[:, cc, :],
                                          start=True, stop=False), reads=[BDT.r, WG.r], writes=[GP.r])
            S.op("pe", lambda e: e.matmul(GP.t[:, cc * 16:(cc + 1) * 16], lhsT=BDT.t[:, 8 + cc, :], rhs=WG.t[:, 8 + cc, :],
                                          start=False, stop=True), reads=[BDT.r, WG.r], writes=[GP.r])
            S.op("pe", lambda e: e.matmul(GP.t[:, 128 + cc * 16:128 + (cc + 1) * 16], lhsT=BDT.t[:, 16 + cc, :],
                                          rhs=WG.t[:, 16 + cc, :], start=True, stop=True), reads=[BDT.r, WG.r], writes=[GP.r])
            yield
        S.op("dve", lambda e: e.tensor_copy(out=WF.t[:, :, :], in_=GP.t[:, 0:256].rearrange("p (k q) -> p k q", q=16)),
             reads=[GP.r], writes=[WF.r])
        yield

    def wdma(j):
        if j < 3:
            c0 = (0, D, 3 * D)[j]
            S.dma("pool", WR.t[:, :, j * D:(j + 1) * D], w_in_v[:, :, c0:c0 + D], writes=[WR.sub(j)])
        elif j == 3:
            S.dma("pool", PW.t[:, :, :], pool_w.rearrange("(k p) d -> p k d", p=128), writes=[PW.r])
            S.dma("pool", BND.t[:, :, :], bands.rearrange("p (k t) -> p k t", t=128), writes=[BND.r])
        else:
            S.dma("pool", WMX.t[:, :, :], w_in_v[:, :, 2 * D:3 * D], writes=[WMX.r])
            S.dma("pool", CD.t[:, :, :], cdiag.rearrange("p (k q) -> p k q", q=128), writes=[CD.r])
            S.dma("pool", BD.t[:, :, :], bd.rearrange("p (k q) -> p k q", q=128), writes=[BD.r])
        yield

    wdma_seq = [wdma(0), wdma(1), wdma(2), wdma(3), wdma(4)]
    run(wdma_seq[0])
    run(ilv(seq(*[g for k in range(5) for g in (p0_block(k, UTD[k % 2]), wdma_seq[k] if k > 0 else None)]), fold_gen()))
    fold_tmp_res = [BDT.r, WG.r]
    fold_tmp.close()
    UTP = sb(ph, "utp", [128, 8, 512], BF16)
    for r in fold_tmp_res:
        UTP.r.readers.update(r.readers)
        if r.last_w is not None:
            UTP.r.readers[("w", r.name)] = r.last_w
    S.dma("act", UTD[1].t[:, :, 0:128], UT_s[:, :, 15 * 128:16 * 128], reads=[ut_res[3]], writes=[UTD[1].r])
    run(px_tile(UTD[1], 0, 15))
    load_ut(UTD[0], 4)
    run(c1_front(0))
    for ob in range(4):
        run(ilv(c1_pool(ob), c1_front(ob + 1) if ob < 3 else None, p0_block(5 + ob, UTP) if ob < 3 else None))
    run(pool_tile(31, True))
    S.barrier()
    ph.close()
    if stop_after == "C1":
        return nc, S

    wo_stack = ExitStack()
    WO = sb(wo_stack, "w_out", [128, 16, D], BF16)
    ph = ExitStack()
    set_psum(tp=0, mm=4, scan=True, sa=0)
    UTD = [sb(ph, f"ut{i}", [128, 8, 512], BF16) for i in range(2)]
    MXT = [sb(ph, f"mxt{i}", [128, 8, 516], BF16) for i in range(2)]
    CTB = sb(ph, "ctb", [128, 8, 512], BF16)
    SCB = sb(ph, "scb", [128, 8, 512], BF16)
    QTB = [sb(ph, f"qtb{i}", [128, 8, 512], BF16) for i in range(2)]
    KTB = [sb(ph, f"ktb{i}", [128, 8, 512], BF16) for i in range(2)]
    NKV = 6
    KT_ = [sb(ph, f"kt{i}", [128, D], BF16) for i in range(NKV)]
    VX = [sb(ph, f"vx{i}", [128, 4, 257], BF16) for i in range(NKV)]
    HF = [sb(ph, f"hf{i}", [128, D], F32) for i in range(2)]
    ZGS = [sb(ph, f"zg{i}", [128, 32], F32) for i in range(4)]
    ST = [sb(ph, f"st{i}", [128, 128], BF16) for i in range(4)]
    KS = [sb(ph, f"ks{i}", [128, 256], BF16) for i in range(2)]
    DN = [sb(ph, f"dn{i}", [128, 4], F32) for i in range(4)]
    CST_ = [sb(ph, f"cstate{h}", [128, 2, 257], F32) for h in range(4)]
    CB = [sb(ph, f"cb{h}", [128, 2, 257], BF16) for h in range(4)]

    def init_state(CST_, CB):
        for h in range(4):
            CST_[h].sub(0), CST_[h].sub(1)
            S.op("pool", lambda e: e.memset(CST_[h].t[:, :, :], 0.0), writes=CST_[h].all())
            S.op("pool", lambda e: e.memset(CB[h].t[:, :, :], 0.0), writes=[CB[h].r])

    init_state(CST_, CB)
    for i in range(NKV):
        S.op("pool", lambda e: e.memset(VX[i].t[:, :, 256:257], 1.0), writes=[VX[i].r])
    S.op("pool", lambda e: e.memset(MXT[0].t[:, :, 0:2], 0.0), writes=[MXT[0].r])

    def gate_tile(T, CTt, ct_r, MXt, mx_r):
        ZG = ZGS[T % 4]
        GP = next_mm()
        for cc in range(8):
            S.op("pe", lambda e: e.matmul(GP.t[:, 0:16], lhsT=CTt(cc), rhs=WF.t[:, cc, :], start=(cc == 0), stop=False),
                 reads=[ct_r, WF.r], writes=[GP.r])
        for cc in range(8):
            S.op("pe", lambda e: e.matmul(GP.t[:, 0:16], lhsT=MXt(cc), rhs=WF.t[:, 8 + cc, :], start=False, stop=(cc == 7)),
                 reads=[mx_r, WF.r], writes=[GP.r])
        S.op("dve", lambda e: e.tensor_tensor(out=ZG.t[:, 0:16], in0=GP.t[:, 0:16], in1=GBIAS, op=ALU.add),
             reads=[GP.r, BC.r], writes=[ZG.r])
        S.op("act", lambda e: e.activation(out=ZG.t[:, 24:32], in_=ZG.t[:, 8:16], func=AF.Exp, scale=-1.0),
             reads=[ZG.r], writes=[ZG.r])
        S.op("dve", lambda e: e.tensor_scalar_add(out=ZG.t[:, 24:32], in0=ZG.t[:, 24:32], scalar1=1.0),
             reads=[ZG.r], writes=[ZG.r])
        S.op("act", lambda e: e.activation(out=ZG.t[:, 16:24], in_=ZG.t[:, 24:32], func=AF.Ln), reads=[ZG.r], writes=[ZG.r])
        yield

    def gate_tile_b(T):
        ZG = ZGS[T % 4]
        g0 = T * 32
        GP = next_mm()
        S.op("pe", lambda e: e.matmul(GP.t[:, 16:20], lhsT=LF, rhs=ZG.t[:, 16:20], start=True, stop=True),
             reads=[CST.r, ZG.r], writes=[GP.r])
        S.op("pe", lambda e: e.matmul(GP.t[:, 20:24], lhsT=LR, rhs=ZG.t[:, 20:24], start=True, stop=True),
             reads=[CST.r, ZG.r], writes=[GP.r])
        S.op("pe", lambda e: e.matmul(GP.t[:, 24:32], lhsT=ONES, rhs=ZG.t[:, 16:24], start=True, stop=True),
             reads=[CST.r, ZG.r], writes=[GP.r])
        gr = gs_res[T]
        S.op("dve", lambda e: e.tensor_tensor(out=ZG.t[:, 24:32], in0=ZG.t[:, 0:8], in1=GP.t[:, 16:24], op=ALU.add),
             reads=[ZG.r, GP.r], writes=[ZG.r])
        S.op("act", lambda e: e.activation(out=GS.t[:, g0:g0 + 8], in_=ZG.t[:, 24:32], func=AF.Exp), reads=[ZG.r], writes=[gr["as"]])
        S.op("act", lambda e: e.activation(out=GS.t[:, g0 + 16:g0 + 24], in_=GP.t[:, 16:24], func=AF.Exp), reads=[GP.r], writes=[gr["fl"]])
        S.op("act", lambda e: e.activation(out=GS.t[:, g0 + 24:g0 + 32], in_=GP.t[:, 24:32], func=AF.Exp, scale=-1.0),
             reads=[GP.r], writes=[gr["dec"]])
        S.op("dve", lambda e: e.tensor_tensor(out=GS.t[:, g0 + 8:g0 + 16], in0=GS.t[:, g0:g0 + 8], in1=GS.t[:, g0 + 24:g0 + 32],
                                              op=ALU.mult), reads=[gr["as"], gr["dec"]], writes=[gr["ae"]])
        yield

    SCAN_ON_DVE = [False]

    def scan_step(bufs, T, d, h, QTt, q_r, KTt, k_r, ktok, vx, want_out, out_cb):
        ST, KS, DN, CST_, CB = bufs
        g0 = T * 32
        gr = gs_res[T]
        col = d * 4 + h
        a_s = GS.t[:, g0 + col:g0 + col + 1]
        a_e = GS.t[:, g0 + 8 + col:g0 + 9 + col]
        fl = GS.t[:, g0 + 16 + col:g0 + 17 + col]
        dec = GS.t[:, g0 + 24 + col:g0 + 25 + col]
        mask = LF if d == 0 else LR
        bbank = SB[h % 2]
        abank = SA[h % len(SA)] if SA else bbank
        A = abank.t[:, 0:128]
        B = bbank.t[:, 128:385]
        st, ks, dn = ST[h], KS[h % len(KS)], DN[h]
        cst_, cb = CST_[h], CB[h]
        if SCAN_ON_DVE[0]:
            S.op("dve", lambda e: e.tensor_scalar(out=ks.t[:], in0=ktok.t[:, h * 256:(h + 1) * 256], scalar1=a_e, scalar2=None,
                                                  op0=ALU.mult), reads=[ktok.r, gr["ae"]], writes=[ks.r])
        else:
            S.op("act", lambda e: e.activation(out=ks.t[:], in_=ktok.t[:, h * 256:(h + 1) * 256], func=AF.Copy, scale=a_e),
                 reads=[ktok.r, gr["ae"]], writes=[ks.r])
        if want_out:
            for dc in range(2):
                S.op("pe", lambda e: e.matmul(A, lhsT=KTt(dc), rhs=QTt(dc), start=(dc == 0), stop=(dc == 1)),
                     reads=[k_r, q_r], writes=[abank.r])
            S.op("dve", lambda e: e.scalar_tensor_tensor(out=st.t[:], in0=A, scalar=a_s, in1=mask, op0=ALU.mult, op1=ALU.mult),
                 reads=[abank.r, gr["as"], CST.r], writes=[st.r])
            yield
            S.op("pe", lambda e: e.matmul(B, lhsT=st.t[:], rhs=vx.t[:, h, :], start=True, stop=False),
                 reads=[st.r, vx.r], writes=[bbank.r])
            for dc in range(2):
                S.op("pe", lambda e: e.matmul(B, lhsT=QTt(dc), rhs=cb.t[:, dc, :], start=False, stop=(dc == 1)),
                     reads=[q_r, cb.r], writes=[bbank.r])
            S.op("dve", lambda e: e.tensor_scalar(out=dn.t[:, 0:1], in0=B[:, 256:257], scalar1=fl, scalar2=None, op0=ALU.max),
                 reads=[bbank.r, gr["fl"]], writes=[dn.r])
        yield
        for dc in range(2):
            S.op("pe", lambda e: e.matmul(PD[dc].t[:, 0:257], lhsT=ks.t[:, dc * 128:(dc + 1) * 128], rhs=vx.t[:, h, :],
                                          start=True, stop=True), reads=[ks.r, vx.r], writes=[PD[dc].r])
        def upd(dc):
            S.op("dve", lambda e: e.scalar_tensor_tensor(out=cst_.t[:, dc, :], in0=cst_.t[:, dc, :], scalar=dec,
                                                          in1=PD[dc].t[:, 0:257], op0=ALU.mult, op1=ALU.add),
                 reads=[cst_.sub(dc), gr["dec"], PD[dc].r], writes=[cst_.sub(dc)])

        if want_out:
            S.op("dve", lambda e: e.scalar_tensor_tensor(out=dn.t[:, 1:2], in0=B[:, 256:257], scalar=-1.0, in1=dn.t[:, 0:1],
                                                          op0=ALU.mult, op1=ALU.max), reads=[bbank.r, dn.r], writes=[dn.r])
            upd(0)
            S.op("dve", lambda e: e.reciprocal(out=dn.t[:, 2:3], in_=dn.t[:, 1:2]), reads=[dn.r], writes=[dn.r])
            upd(1)
        else:
            upd(0)
            upd(1)
        if SCAN_ON_DVE[0]:
            S.op("dve", lambda e: e.tensor_copy(out=cb.t[:, :, :], in_=cst_.t[:, :, :]), reads=cst_.all(), writes=[cb.r])
        else:
            S.op("act", lambda e: e.activation(out=cb.t[:, :, :], in_=cst_.t[:, :, :], func=AF.Copy), reads=cst_.all(), writes=[cb.r])
        if want_out:
            out_cb(h, B, dn.t[:, 2:3], bbank.r, dn.r)
        yield

    def s1(b):
        UT = UTD[b % 2]
        if b + 1 < 8:
            load_ut(UTD[(b + 1) % 2], b + 1)
        yield
        mx = MXT[b % 2]
        yield from proj_fm(WMX, 0, 8, UT, 512, lambda cc, ps: S.op(
            "act", lambda e: e.activation(out=mx.t[:, cc, 2:514], in_=ps.t[:, 0:512], func=AF.Copy), reads=[ps.r], writes=[mx.r]),
            fine=False)
        if b > 0:
            pm = MXT[(b - 1) % 2]
            S.op("pool", lambda e: e.tensor_copy(out=pm.t[:, :, 514:516], in_=mx.t[:, :, 2:4]), reads=[mx.r], writes=[pm.r])
            S.op("pool", lambda e: e.tensor_copy(out=mx.t[:, :, 0:2], in_=pm.t[:, :, 512:514]), reads=[pm.r], writes=[mx.r])
        if b == 7:
            S.op("pool", lambda e: e.memset(mx.t[:, :, 514:516], 0.0), writes=[mx.r])

    def s2(b):
        own = b >= 4
        mx = MXT[b % 2]
        qtb, ktb = QTB[b % 2], KTB[b % 2]
        for cc in range(8):
            ps = next_mm()
            for j in range(5):
                S.op("pe", lambda e: e.matmul(ps.t[:, 0:512], lhsT=CD.t[:, cc * 5 + j, :], rhs=mx.t[:, cc, j:j + 512],
                                              start=(j == 0), stop=(j == 4)), reads=[CD.r, mx.r], writes=[ps.r])
            S.op("act", lambda e: e.activation(out=CTB.t[:, cc, :], in_=ps.t[:, 0:512], func=AF.Silu, bias=CONVB(cc)),
                 reads=[ps.r, VEC.r], writes=[CTB.r])
            if own:
                S.op("act", lambda e: e.activation(out=SCB.t[:, cc, :], in_=CTB.t[:, cc, :], func=AF.Copy, scale=SKIP(cc)),
                     reads=[CTB.r, VEC.r], writes=[SCB.r])
            yield
        if own:
            o0 = (b - 4) * 512
            S.dma("sp", SC_s[:, :, o0:o0 + 512], SCB.t[:, :, :], reads=[SCB.r], writes=[r_scr["SC"]])
            for cc in range(8):
                ps = next_mm()
                S.op("pe", lambda e: e.matmul(ps.t[:, 0:512], lhsT=BD.t[:, cc, :], rhs=CTB.t[:, cc, :], start=True, stop=True),
                     reads=[BD.r, CTB.r], writes=[ps.r])
                S.op("act", lambda e: e.activation(out=qtb.t[:, cc, :], in_=ps.t[:, 0:512], func=AF.Copy), reads=[ps.r], writes=[qtb.r])
                ps = next_mm()
                S.op("pe", lambda e: e.matmul(ps.t[:, 0:512], lhsT=BD.t[:, 8 + cc, :], rhs=CTB.t[:, cc, :], start=True, stop=True),
                     reads=[BD.r, CTB.r], writes=[ps.r])
                S.op("act", lambda e: e.activation(out=ktb.t[:, cc, :], in_=ps.t[:, 0:512], func=AF.Copy, scale=1.0 / 16),
                     reads=[ps.r], writes=[ktb.r])
                yield
            S.dma("sp", QT_s[:, :, o0:o0 + 512], qtb.t[:, :, :], reads=[qtb.r], writes=[r_scr["QT"]])
            S.dma("sp", KT_s[:, :, o0:o0 + 512], ktb.t[:, :, :], reads=[ktb.r], writes=[r_scr["KT"]])
        for i in range(4):
            T = b * 4 + i
            kt, vx = KT_[T % NKV], VX[T % NKV]
            for half in range(2):
                ps = next_mm()
                for c4 in range(4):
                    cc = half * 4 + c4
                    S.op("pe", lambda e: e.matmul(ps.t[:, c4 * 128:(c4 + 1) * 128], lhsT=CTB.t[:, cc, i * 128:(i + 1) * 128],
                                                  rhs=BD.t[:, 8 + cc, :], start=True, stop=True),
                         reads=[CTB.r, BD.r], writes=[ps.r])
                S.op("act", lambda e: e.activation(out=kt.t[:, half * 512:(half + 1) * 512], in_=ps.t[:, 0:512], func=AF.Copy,
                                                   scale=1.0 / 16), reads=[ps.r], writes=[kt.r])
                ps = next_mm()
                for c4 in range(4):
                    cc = half * 4 + c4
                    S.op("pe", lambda e: e.matmul(ps.t[:, c4 * 128:(c4 + 1) * 128], lhsT=mx.t[:, cc, 2 + i * 128:2 + (i + 1) * 128],
                                                  rhs=BD.t[:, 16 + cc, :], start=True, stop=True),
                         reads=[mx.r, BD.r], writes=[ps.r])
                S.op("act", lambda e: e.activation(out=vx.t[:, half * 2:(half + 1) * 2, 0:256],
                                                   in_=ps.t[:, 0:512].rearrange("p (h e) -> p h e", e=256), func=AF.Copy),
                     reads=[ps.r], writes=[vx.r])
                yield
            yield from gate_tile(T, lambda cc: CTB.t[:, cc, i * 128:(i + 1) * 128], CTB.r,
                                 lambda cc: mx.t[:, cc, 2 + i * 128:2 + (i + 1) * 128], mx.r)
            if own:
                S.dma("sp", K_s[T - 16, :, :], kt.t[:], reads=[kt.r], writes=[r_scr["K"]])
                S.dma("sp", V_s[T - 16, :, :].rearrange("p (h e) -> p h e", e=256), vx.t[:, :, 0:256], reads=[vx.r],
                      writes=[r_scr["V"]])
        for i in range(4):
            yield from gate_tile_b(b * 4 + i)

    def fscan(b):
        own = b >= 4
        qtb, ktb = QTB[b % 2], KTB[b % 2]
        bufs = (ST, KS, DN, CST_, CB)
        for i in range(4):
            T = b * 4 + i
            kt, vx = KT_[T % NKV], VX[T % NKV]
            hf = HF[T % 2]

            def out_cb(h, B, r, b_r, dn_r, hf=hf):
                if SCAN_ON_DVE[0]:
                    S.op("dve", lambda e: e.tensor_scalar(out=hf.t[:, h * 256:(h + 1) * 256], in0=B[:, 0:256], scalar1=r,
                                                          scalar2=None, op0=ALU.mult), reads=[b_r, dn_r], writes=[hf.sub(h)])
                else:
                    S.op("act", lambda e: e.activation(out=hf.t[:, h * 256:(h + 1) * 256], in_=B[:, 0:256], func=AF.Copy,
                                                       scale=r), reads=[b_r, dn_r], writes=[hf.sub(h)])

            def step(h, i=i, T=T, kt=kt, vx=vx, out_cb=out_cb):
                return scan_step(bufs, T, 0, h,
                                 lambda dc: qtb.t[:, 2 * h + dc, i * 128:(i + 1) * 128], qtb.r,
                                 lambda dc: ktb.t[:, 2 * h + dc, i * 128:(i + 1) * 128], ktb.r,
                                 kt, vx, own, out_cb)

            if b == 7:
                SCAN_ON_DVE[0] = False
            yield from ilv(step(0), step(1))
            yield from ilv(step(2), step(3))
            if own:
                S.dma("sp", HF_s[T - 16, :, :], hf.t[:], reads=hf.all(), writes=[r_scr["HF"]])

    SCAN_ON_DVE[0] = True
    load_ut(UTD[0], 0)
    run(s1(0))
    run(s1(1))
    run(s2(0))
    for b in range(8):
        if b == 4:
            S.dma("pool", WO.t[:, :, :], w_out.rearrange("(k p) d -> p k d", p=128), writes=[WO.r])
        run(ilv(fscan(b), seq(s1(b + 2) if b + 2 < 8 else None, s2(b + 1) if b + 1 < 8 else None)))
    if DEBUG:
        S.dma("sp", GS_s[:, :], GS.t[:], reads=[r for g in gs_res for r in g.values()], writes=[r_scr["GS"]])
    S.barrier()
    ph.close()
    SCAN_ON_DVE[0] = False
    if stop_after == "AB":
        return nc, S

    ph = ExitStack()
    set_psum(tp=1, mm=3, scan=True, sa=0)
    NS, NA, NB = 3, 3, 4
    XT = [sb(ph, f"xt{i}", [128, D], F32) for i in range(NB)]
    QT2 = [sb(ph, f"qt{i}", [128, 8, 128], BF16) for i in range(NS)]
    KT2 = [sb(ph, f"ktt{i}", [128, 8, 128], BF16) for i in range(NS)]
    SC2 = [sb(ph, f"sc{i}", [128, 8, 128], BF16) for i in range(NA)]
    SMZ2 = [sb(ph, f"smz{i}", [128, 8, 128], BF16) for i in range(NA)]
    YP2 = [sb(ph, f"yp{i}", [128, 8, 128], BF16) for i in range(NB)]
    K2 = [sb(ph, f"k{i}", [128, D], BF16) for i in range(NS)]
    VX2 = [sb(ph, f"vx{i}", [128, 4, 257], BF16) for i in range(NS)]
    HF2 = [sb(ph, f"hf{i}", [128, D], F32) for i in range(NS)]
    HH = [sb(ph, f"hh{i}", [128, D], F32) for i in range(2)]
    HN = sb(ph, "hn", [128, D], BF16)
    T2 = sb(ph, "t2", [128, 8, 128], F32)
    YM = [sb(ph, f"ym{i}", [128, 8, 128], BF16) for i in range(2)]
    OO = sb(ph, "oo", [128, D], F32)
    OS = sb(ph, "os", [128, D], F32)
    OUTT = [sb(ph, f"outt{i}", [128, D], F32) for i in range(2)]
    BS = sb(ph, "bs", [128, 4, 6], F32)
    MV = sb(ph, "mv", [128, 4, 2], F32)
    RS = sb(ph, "rs", [128, 12], F32)
    ST = [sb(ph, f"st{i}", [128, 128], BF16) for i in range(4)]
    KS = [sb(ph, f"ks{i}", [128, 256], BF16) for i in range(4)]
    DN = [sb(ph, f"dn{i}", [128, 4], F32) for i in range(4)]
    CST_ = [sb(ph, f"cstate{h}", [128, 2, 257], F32) for h in range(4)]
    CB = [sb(ph, f"cb{h}", [128, 2, 257], BF16) for h in range(4)]

    init_state(CST_, CB)
    for i in range(NS):
        S.op("pool", lambda e: e.memset(VX2[i].t[:, :, 256:257], 1.0), writes=[VX2[i].r])

    def load_tile(T):
        s = T % NS
        o0 = (T - 16) * 128
        S.dma("sp", QT2[s].t[:, :, :], QT_s[:, :, o0:o0 + 128], reads=[r_scr["QT"]], writes=[QT2[s].r])
        S.dma("sp", KT2[s].t[:, :, :], KT_s[:, :, o0:o0 + 128], reads=[r_scr["KT"]], writes=[KT2[s].r])
        S.dma("sp", K2[s].t[:], K_s[T - 16, :, :], reads=[r_scr["K"]], writes=[K2[s].r])
        S.dma("sp", VX2[s].t[:, :, 0:256], V_s[T - 16, :, :].rearrange("p (h e) -> p h e", e=256), reads=[r_scr["V"]],
              writes=[VX2[s].r])
        S.dma("sp", HF2[s].t[:], HF_s[T - 16, :, :], reads=[r_scr["HF"]], writes=HF2[s].all())
        a, b = T % NA, T % NB
        S.dma("sp", SC2[a].t[:, :, :], SC_s[:, :, o0:o0 + 128], reads=[r_scr["SC"]], writes=[SC2[a].r])
        S.dma("sp", SMZ2[a].t[:, :, :], SMZ_s[:, :, o0:o0 + 128], reads=[r_scr["SMZ"]], writes=[SMZ2[a].r])
        S.dma("sp", YP2[b].t[:, :, :], YP_s[:, :, o0:o0 + 128], reads=[r_scr["YP"]], writes=[YP2[b].r])
        S.dma("sp", XT[b].t[:], xl[T * 128:(T + 1) * 128, :], writes=[XT[b].r])

    def rscan(T):
        s = T % NS
        qt, ktt, kk, vx, hf, hh = QT2[s], KT2[s], K2[s], VX2[s], HF2[s], HH[T % 2]
        bufs = (ST, KS, DN, CST_, CB)

        def out_cb(h, B, r, b_r, dn_r):
            S.op("dve", lambda e: e.scalar_tensor_tensor(out=hh.t[:, h * 256:(h + 1) * 256], in0=B[:, 0:256], scalar=r,
                                                          in1=hf.t[:, h * 256:(h + 1) * 256], op0=ALU.mult, op1=ALU.add),
                 reads=[b_r, dn_r, hf.sub(h)], writes=[hh.sub(h)])

        def step(h):
            return scan_step(bufs, T, 1, h, lambda dc: qt.t[:, 2 * h + dc, :], qt.r, lambda dc: ktt.t[:, 2 * h + dc, :], ktt.r,
                             kk, vx, True, out_cb)

        yield from ilv(step(0), step(1))
        yield from ilv(step(2), step(3))

    def final_a(T):
        sc, smz, hh, ym = SC2[T % NA], SMZ2[T % NA], HH[T % 2], YM[T % 2]
        for h in range(4):
            S.op("dve", lambda e: e.bn_stats(out=BS.t[:, h, :], in_=hh.t[:, h * 256:(h + 1) * 256]), reads=[hh.sub(h)], writes=[BS.sub(h)])
            if h % 2 == 1:
                yield
        for h in range(4):
            S.op("dve", lambda e: e.bn_aggr(out=MV.t[:, h, :], in_=BS.t[:, h, :]), reads=[BS.sub(h)], writes=[MV.sub(h)])
        S.op("pool", lambda e: e.tensor_scalar_add(out=RS.t[:, 0:4], in0=MV.t[:, :, 1], scalar1=EPS), reads=MV.all(), writes=[RS.r])
        S.op("pool", lambda e: e.tensor_tensor(out=RS.t[:, 4:8], in0=RS.t[:, 0:4], in1=MH.t[:, 0:4], op=ALU.pow),
             reads=[RS.r, MH.r], writes=[RS.r])
        S.op("dve", lambda e: e.scalar_tensor_tensor(out=RS.t[:, 8:12], in0=MV.t[:, :, 0], scalar=-1.0, in1=RS.t[:, 4:8],
                                                      op0=ALU.mult, op1=ALU.mult), reads=MV.all() + [RS.r], writes=[RS.r])
        yield
        for h in range(4):
            S.op("act", lambda e: e.activation(out=HN.t[:, h * 256:(h + 1) * 256], in_=hh.t[:, h * 256:(h + 1) * 256],
                                               func=AF.Identity, scale=RS.t[:, 4 + h:5 + h], bias=RS.t[:, 8 + h:9 + h]),
                 reads=[hh.sub(h), RS.r], writes=[HN.sub(h)])
            if h % 2 == 1:
                yield
        tp = next_tp()
        for cc in range(8):
            S.op("pe", lambda e: e.transpose(tp.t[:, cc, :], HN.t[:, cc * 128:(cc + 1) * 128], IDB.t[:]),
                 reads=[HN.sub(cc // 2), IDB.r], writes=[tp.r])
        for cc in range(8):
            S.op("act", lambda e: e.activation(out=T2.t[:, cc, :], in_=tp.t[:, cc, :], func=AF.Copy, scale=WN(cc)),
                 reads=[tp.r, VEC.r], writes=[T2.sub(cc)])
        S.op("dve", lambda e: e.tensor_tensor(out=T2.t[:, :, :], in0=T2.t[:, :, :], in1=sc.t[:, :, :], op=ALU.add),
             reads=T2.all() + [sc.r], writes=T2.all())
        S.op("pool", lambda e: e.tensor_tensor(out=ym.t[:, :, :], in0=T2.t[:, :, :], in1=smz.t[:, :, :], op=ALU.mult),
             reads=T2.all() + [smz.r], writes=[ym.r])
        yield

    def final_b(T):
        yp, xt, ym = YP2[T % NB], XT[T % NB], YM[T % 2]
        for half in range(2):
            ps = next_mm()
            for k in range(16):
                lhs = yp.t[:, k, :] if k < 8 else ym.t[:, k - 8, :]
                S.op("pe", lambda e: e.matmul(ps.t[:, 0:512], lhsT=lhs, rhs=WO.t[:, k, half * 512:(half + 1) * 512],
                                              start=(k == 0), stop=(k == 15)), reads=[yp.r, ym.r, WO.r], writes=[ps.r])
                if k % 2 == 1 and k < 15:
                    yield
            S.op("dve", lambda e: e.tensor_tensor(out=OO.t[:, half * 512:(half + 1) * 512], in0=ps.t[:, 0:512],
                                                  in1=xt.t[:, half * 512:(half + 1) * 512], op=ALU.add),
                 reads=[ps.r, xt.r], writes=[OO.sub(half)])
            yield
        rms_scale(OO, 3)
        ot = OUTT[T % 2]
        S.op("act", lambda e: e.activation(out=OS.t[:], in_=OO.t[:], func=AF.Copy, scale=SS.t[:, 5:6]),
             reads=OO.all() + [SS.r], writes=[OS.r])
        S.op("pool", lambda e: e.tensor_tensor(out=ot.t[:], in0=OS.t[:], in1=G_OUT, op=ALU.mult),
             reads=[OS.r, BC.r], writes=[ot.r])
        S.dma("sp", out[(T - 16) * 128:(T - 15) * 128, :], ot.t[:], reads=[ot.r], writes=[r_scr["out"]])
        yield

    load_tile(31)
    load_tile(30)
    for T in range(31, 13, -1):
        if 16 < T <= 30:
            load_tile(T - 1)
        run(ilvw((rscan(T) if T >= 16 else None, 1),
                 (final_a(T + 1) if 16 <= T + 1 <= 31 else None, 1),
                 (final_b(T + 2) if 16 <= T + 2 <= 31 else None, 2)))
    S.barrier()
    ph.close()
    wo_stack.close()
    abw.close()
    return nc, S


def _block_diag(w):
    o = np.zeros((8, 128, 128), np.float32)
    for cc in range(8):
        for n in range(32):
            o[cc, 4 * n:4 * n + 4, 4 * n:4 * n + 4] = w[32 * cc + n]
    return o


def _bands(flip):
    o = np.zeros((4, 4, 128, 128), np.float32)
    for g, w in enumerate(POOL_WINDOWS):
        left = (w - 1) // 2
        right = w - 1 - left
        if flip:
            left, right = right, left
        for kind, rel, end in ((0, -1, False), (1, 0, False), (2, 1, False), (3, 0, True)):
            for t in range(128):
                lo, hi = t - left, t + right
                if end:
                    hi = min(hi, 127)
                cntv = hi - lo + 1
                for sl in range(128):
                    s = rel * 128 + sl
                    v = 0.0
                    if lo <= s <= hi:
                        v = 1.0 / cntv
                    if rel == 0 and s == t:
                        v -= 1.0
                    o[g, kind, sl, t] = v
    return np.ascontiguousarray(np.transpose(o, (2, 0, 1, 3)).reshape(128, 16 * 128))


def _pvec(v):
    return np.ascontiguousarray(np.asarray(v, np.float32).reshape(8, 128).T)


_CACHE = {}


def _prep_inputs(x, norm_in_g, w_in, pool_w, pool_scale, conv_w, conv_b, w_q, w_k, w_v, w_gates, b_gates,
                 mh_norm_w, skip_w, w_out, norm_out_g):
    f = lambda a: np.ascontiguousarray(np.asarray(a, np.float32))
    x = f(x)
    w_in0, w_out0, pool_w0 = f(w_in)[0], f(w_out)[0], f(pool_w)[0].reshape(1024, 256)
    conv_w0, w_g, b_g = f(conv_w)[0], f(w_gates)[0], f(b_gates)[0]
    bdq, bdk, bdv = _block_diag(f(w_q)[0]), _block_diag(f(w_k)[0]), _block_diag(f(w_v)[0])
    bd = np.concatenate([bdq, bdk, bdv], 0)
    bd_h = np.ascontiguousarray(np.transpose(bd, (1, 0, 2)).reshape(128, 24 * 128))
    bdT_h = np.ascontiguousarray(np.transpose(bd, (2, 0, 1)).reshape(128, 24 * 128))
    vecs = np.concatenate([_pvec(f(pool_scale)[0]), _pvec(f(conv_b)[0]), _pvec(f(mh_norm_w)[0]), _pvec(f(skip_w)[0])], 1)
    tri = np.tril(np.ones((128, 128), np.float32))
    LFm, LRm = np.ascontiguousarray(tri.T), tri
    cst = np.ascontiguousarray(np.concatenate([LFm, LRm, np.ones((128, 128), np.float32)], 1))
    ident = np.eye(128, dtype=np.float32)
    maps = []
    for core in range(8):
        b, half = core // 2, core % 2
        flip = half == 0
        dF = 1 if flip else 0
        dR = 1 - dF
        xl = x[b][::-1] if flip else x[b]
        cw = conv_w0[::-1] if flip else conv_w0
        cd = np.zeros((128, 40, 128), np.float32)
        idx = np.arange(128)
        for cc in range(8):
            for j in range(5):
                cd[idx, cc * 5 + j, idx] = cw[j, cc * 128:(cc + 1) * 128]
        wg16 = np.concatenate([w_g[dF][:, 0:4], w_g[dR][:, 0:4], w_g[dF][:, 4:8], w_g[dR][:, 4:8]], 1)
        wg_h = np.ascontiguousarray(np.transpose(wg16.reshape(24, 128, 16), (1, 0, 2)).reshape(128, 24 * 16))
        bias16 = np.concatenate([b_g[dF][0:4], b_g[dR][0:4], b_g[dF][4:8], b_g[dR][4:8]])
        bcast = np.concatenate([np.tile(f(norm_in_g)[0][None, :], (128, 1)), np.tile(f(norm_out_g)[None, :], (128, 1)),
                                np.tile(bias16[None, :], (128, 1))], 1)
        maps.append({
            "xl": np.ascontiguousarray(xl), "w_in": w_in0, "w_out": w_out0, "pool_w": pool_w0,
            "cdiag": np.ascontiguousarray(cd.reshape(128, 40 * 128)), "bd": bd_h, "bdT": bdT_h, "wg": wg_h,
            "vecs": np.ascontiguousarray(vecs), "bcast": np.ascontiguousarray(bcast.astype(np.float32)), "cst": cst,
            "ident": ident, "bands": _bands(flip),
        })
    return maps


def kernel(**inputs):
    if "nc" not in _CACHE:
        _CACHE["nc"] = build_program()[0]
    nc = _CACHE["nc"]
    maps = _prep_inputs(**inputs)
    res = run_bass_kernel_spmd(nc, maps, core_ids=list(range(8)))
    outp = np.empty((4, SEQ, D), np.float32)
    for core in range(8):
        b, half = core // 2, core % 2
        y = res.results[core]["out"]
        if half == 0:
            outp[b, 0:NOWN] = y[::-1]
        else:
            outp[b, NOWN:SEQ] = y
    if DEBUG:
        _CACHE["last"] = res.results
    return outp
```
